# Optimizing a Trainium2 kernel written in Bass

```python
import math
import jax, jax.numpy as jnp
from jax import lax
import numpy as np

D_MODEL = 1024
BATCH = 8
SEQ = 2048
DEPTH = 4

D_INNER = 2 * D_MODEL
CONV_WIDTH = 4
EPS = 1e-6
LRU_WIDTH = D_INNER // 4
LRU_HEADS = 8
LRU_HEAD_DIM = LRU_WIDTH // LRU_HEADS
LRU_C = 8.0
HG_WIDTH = D_INNER // 4
HG_HEAD_DIM = 128
HG_HEADS = HG_WIDTH // HG_HEAD_DIM
HG_CHUNK = 64
SSD_WIDTH = D_INNER // 2
SSD_HEAD_DIM = 64
SSD_HEADS = SSD_WIDTH // SSD_HEAD_DIM
SSD_GROUPS = 2
SSD_STATE = 128
SSD_CHUNK = 128
SSD_CONV_DIM = SSD_WIDTH + 2 * SSD_GROUPS * SSD_STATE
SPLIT_SIZES = (LRU_WIDTH, LRU_WIDTH,
               HG_WIDTH, HG_WIDTH, HG_WIDTH, HG_WIDTH,
               SSD_WIDTH, SSD_CONV_DIM, SSD_HEADS)
N_IN = sum(SPLIT_SIZES)

kernel_name = "hybrid_rglru_hgrn2_ssd_parallel_heads"


def _split_points():
    return [int(v) for v in np.cumsum(SPLIT_SIZES)[:-1]]


def rmsnorm(x, w):
    xf = x.astype(jnp.float32)
    inv = lax.rsqrt(jnp.mean(xf * xf, axis=-1, keepdims=True) + EPS)
    return (xf * inv).astype(x.dtype) * w


def causal_conv(x, w, b):
    K = w.shape[0]
    S = x.shape[1]
    xp = jnp.pad(x, ((0, 0), (K - 1, 0), (0, 0)))
    out = b
    for k in range(K):
        out = out + xp[:, k:k + S] * w[k]
    return out


def rg_lru(x, wa, ba, wx, bx, lam):
    B, S, _ = x.shape
    xh = x.reshape(B, S, LRU_HEADS, LRU_HEAD_DIM)
    r = jax.nn.sigmoid(jnp.einsum('bshi,hij->bshj', xh, wa) + ba).reshape(B, S, LRU_WIDTH)
    i = jax.nn.sigmoid(jnp.einsum('bshi,hij->bshj', xh, wx) + bx).reshape(B, S, LRU_WIDTH)
    log_a = -LRU_C * r * jax.nn.softplus(-lam)
    a = jnp.exp(log_a)
    mult = jnp.sqrt(-jnp.expm1(2.0 * log_a))
    b = mult * (i * x)

    def combine(left, right):
        a1, b1 = left
        a2, b2 = right
        return a1 * a2, a2 * b1 + b2

    _, h = lax.associative_scan(combine, (a, b), axis=1)
    return h


def hgrn2_chunked(q, k, v, log_f):
    dtype = v.dtype
    q, k, v, log_f = (t.astype(jnp.float32) for t in (q, k, v, log_f))
    B, S, H, DK = q.shape
    DV = v.shape[-1]
    n = S // HG_CHUNK

    def to_chunks(t):
        return t.reshape(B, n, HG_CHUNK, H, t.shape[-1]).transpose(1, 0, 3, 2, 4)

    qc, kc, vc, lc = (to_chunks(t) for t in (q, k, v, log_f))
    causal = jnp.tril(jnp.ones((HG_CHUNK, HG_CHUNK), bool))[:, :, None]

    def step(state, inp):
        qi, ki, vi, li = inp
        cum = jnp.cumsum(li, axis=2)
        diff = cum[:, :, :, None, :] - cum[:, :, None, :, :]
        decay = jnp.where(causal, jnp.exp(jnp.where(causal, diff, 0.0)), 0.0)
        scores = jnp.einsum('bhtd,bhsd,bhtsd->bhts', qi, ki, decay)
        o = (jnp.einsum('bhts,bhsv->bhtv', scores, vi)
             + jnp.einsum('bhtd,bhdv->bhtv', qi * jnp.exp(cum), state))
        last = cum[:, :, -1:, :]
        state = (state * jnp.exp(last[:, :, 0, :, None])
                 + jnp.einsum('bhsd,bhsv->bhdv', ki * jnp.exp(last - cum), vi))
        return state, o

    s0 = jnp.zeros((B, H, DK, DV), jnp.float32)
    _, o = lax.scan(step, s0, (qc, kc, vc, lc))
    return o.transpose(1, 0, 3, 2, 4).reshape(B, S, H, DV).astype(dtype)


def ssd_chunked(x, dt, A, Bm, Cm):
    dtype = x.dtype
    x, dt, A, Bm, Cm = (t.astype(jnp.float32) for t in (x, dt, A, Bm, Cm))
    B, S, H, P = x.shape
    G, N = Bm.shape[2], Bm.shape[3]
    J = H // G
    C = SSD_CHUNK
    n = S // C
    xdt = (x * dt[..., None]).reshape(B, n, C, G, J, P).transpose(1, 0, 2, 3, 4, 5)
    dA = (dt * A).reshape(B, n, C, G, J).transpose(1, 0, 2, 3, 4)
    Bc = Bm.reshape(B, n, C, G, N).transpose(1, 0, 2, 3, 4)
    Cc = Cm.reshape(B, n, C, G, N).transpose(1, 0, 2, 3, 4)
    causal = jnp.tril(jnp.ones((C, C), bool))[:, :, None, None]

    def step(state, inp):
        xi, ai, bi, ci = inp
        cum = jnp.cumsum(ai, axis=1)
        diff = cum[:, :, None] - cum[:, None, :]
        L = jnp.where(causal, jnp.exp(jnp.where(causal, diff, 0.0)), 0.0)
        cb = jnp.einsum('btgn,bsgn->btsg', ci, bi)
        y = jnp.einsum('btsg,btsgj,bsgjp->btgjp', cb, L, xi)
        y = y + jnp.einsum('btgn,bgjpn,btgj->btgjp', ci, state, jnp.exp(cum))
        last = cum[:, -1]
        state = (state * jnp.exp(last)[..., None, None]
                 + jnp.einsum('bsgn,bsgj,bsgjp->bgjpn', bi, jnp.exp(last[:, None] - cum), xi))
        return state, y

    s0 = jnp.zeros((B, G, J, P, N), jnp.float32)
    _, y = lax.scan(step, s0, (xdt, dA, Bc, Cc))
    return y.transpose(1, 0, 2, 3, 4, 5).reshape(B, S, H, P).astype(dtype)


def setup_inputs(seed: int = 0) -> dict:
    key = jax.random.key(seed)
    ks = jax.random.split(key, 24)
    f32 = jnp.float32

    def nrm(k, shape, scale):
        return jax.random.normal(k, shape, f32) * scale

    x = nrm(ks[0], (BATCH, SEQ, D_MODEL), 1.0)
    c = nrm(ks[1], (BATCH, D_MODEL), 1.0)
    norm_w = 1.0 + nrm(ks[2], (DEPTH, D_MODEL), 0.01)
    w_ada = nrm(ks[3], (DEPTH, D_MODEL, 3 * D_MODEL), 0.5 * D_MODEL ** -0.5)
    b_ada = nrm(ks[4], (DEPTH, 3 * D_MODEL), 0.01)
    w_in = nrm(ks[5], (DEPTH, D_MODEL, N_IN), D_MODEL ** -0.5)
    lru_conv_w = nrm(ks[6], (DEPTH, CONV_WIDTH, LRU_WIDTH), CONV_WIDTH ** -0.5)
    lru_conv_b = nrm(ks[7], (DEPTH, LRU_WIDTH), 0.01)
    lru_wa = nrm(ks[8], (DEPTH, LRU_HEADS, LRU_HEAD_DIM, LRU_HEAD_DIM), LRU_HEAD_DIM ** -0.5)
    lru_ba = nrm(ks[9], (DEPTH, LRU_HEADS, LRU_HEAD_DIM), 0.01)
    lru_wx = nrm(ks[10], (DEPTH, LRU_HEADS, LRU_HEAD_DIM, LRU_HEAD_DIM), LRU_HEAD_DIM ** -0.5)
    lru_bx = nrm(ks[11], (DEPTH, LRU_HEADS, LRU_HEAD_DIM), 0.01)
    u = jax.random.uniform(ks[12], (DEPTH, LRU_WIDTH), f32, 0.9, 0.999) ** (1.0 / LRU_C)
    lru_lambda = jnp.log(u) - jnp.log1p(-u)
    hg_lb_logits = nrm(ks[13], (DEPTH, HG_WIDTH), 0.5)
    hg_norm_w = 1.0 + nrm(ks[14], (DEPTH, HG_WIDTH), 0.01)
    ssd_conv_w = nrm(ks[15], (DEPTH, CONV_WIDTH, SSD_CONV_DIM), CONV_WIDTH ** -0.5)
    ssd_conv_b = nrm(ks[16], (DEPTH, SSD_CONV_DIM), 0.01)
    dt0 = jnp.exp(jax.random.uniform(ks[17], (DEPTH, SSD_HEADS), f32, math.log(1e-3), math.log(1e-1)))
    ssd_dt_bias = dt0 + jnp.log(-jnp.expm1(-dt0))
    ssd_a_log = jnp.log(jax.random.uniform(ks[18], (DEPTH, SSD_HEADS), f32, 1.0, 16.0))
    ssd_d = 1.0 + nrm(ks[19], (DEPTH, SSD_HEADS), 0.01)
    ssd_norm_w = 1.0 + nrm(ks[20], (DEPTH, SSD_WIDTH), 0.01)
    w_out = nrm(ks[21], (DEPTH, D_INNER, D_MODEL), D_INNER ** -0.5)
    final_norm_w = 1.0 + nrm(ks[22], (D_MODEL,), 0.01)
    return {"x": x, "c": c, "norm_w": norm_w, "w_ada": w_ada, "b_ada": b_ada,
            "w_in": w_in, "lru_conv_w": lru_conv_w, "lru_conv_b": lru_conv_b,
            "lru_wa": lru_wa, "lru_ba": lru_ba, "lru_wx": lru_wx, "lru_bx": lru_bx,
            "lru_lambda": lru_lambda, "hg_lb_logits": hg_lb_logits, "hg_norm_w": hg_norm_w,
            "ssd_conv_w": ssd_conv_w, "ssd_conv_b": ssd_conv_b, "ssd_dt_bias": ssd_dt_bias,
            "ssd_a_log": ssd_a_log, "ssd_d": ssd_d, "ssd_norm_w": ssd_norm_w,
            "w_out": w_out, "final_norm_w": final_norm_w}


def reference(x, c, norm_w, w_ada, b_ada, w_in, lru_conv_w, lru_conv_b, lru_wa, lru_ba,
              lru_wx, lru_bx, lru_lambda, hg_lb_logits, hg_norm_w, ssd_conv_w, ssd_conv_b,
              ssd_dt_bias, ssd_a_log, ssd_d, ssd_norm_w, w_out, final_norm_w):
    B, S, _ = x.shape
    split_points = _split_points()
    cond = jax.nn.silu(c)
    p = jax.nn.softmax(hg_lb_logits.astype(jnp.float32), axis=0)
    lower_bounds = jnp.cumsum(p, axis=0) - p[0]

    for l in range(DEPTH):
        mod = cond @ w_ada[l] + b_ada[l]
        shift, scale, gate = jnp.split(mod, 3, axis=-1)
        h = rmsnorm(x, norm_w[l]) * (1.0 + scale[:, None]) + shift[:, None]
        u = h @ w_in[l]
        a_x, a_g, hg_q, hg_f, hg_i, hg_g, ssd_z, ssd_xbc, ssd_dt = jnp.split(u, split_points, axis=-1)

        xa = causal_conv(a_x, lru_conv_w[l], lru_conv_b[l])
        ya = rg_lru(xa, lru_wa[l], lru_ba[l], lru_wx[l], lru_bx[l], lru_lambda[l]) * jax.nn.silu(a_g)

        lb = lower_bounds[l]
        hf = hg_f.astype(jnp.float32)
        f = lb + (1.0 - lb) * jax.nn.sigmoid(hf)
        log_f = jnp.log(f)
        k_in = (1.0 - lb) * jax.nn.sigmoid(-hf)
        q_h = jax.nn.silu(hg_q).reshape(B, S, HG_HEADS, HG_HEAD_DIM)
        o_b = hgrn2_chunked(q_h, k_in.reshape(B, S, HG_HEADS, HG_HEAD_DIM),
                            hg_i.reshape(B, S, HG_HEADS, HG_HEAD_DIM),
                            log_f.reshape(B, S, HG_HEADS, HG_HEAD_DIM))
        yb = rmsnorm(o_b.reshape(B, S, HG_WIDTH), hg_norm_w[l]) * jax.nn.silu(hg_g)

        xbc = jax.nn.silu(causal_conv(ssd_xbc, ssd_conv_w[l], ssd_conv_b[l]))
        xs, Bm, Cm = jnp.split(xbc, [SSD_WIDTH, SSD_WIDTH + SSD_GROUPS * SSD_STATE], axis=-1)
        dt = jax.nn.softplus(ssd_dt + ssd_dt_bias[l])
        A = -jnp.exp(ssd_a_log[l])
        xs_h = xs.reshape(B, S, SSD_HEADS, SSD_HEAD_DIM)
        yc = ssd_chunked(xs_h, dt, A, Bm.reshape(B, S, SSD_GROUPS, SSD_STATE),
                         Cm.reshape(B, S, SSD_GROUPS, SSD_STATE))
        yc = (yc + ssd_d[l][:, None] * xs_h).reshape(B, S, SSD_WIDTH) * jax.nn.silu(ssd_z)
        yc = rmsnorm(yc.reshape(B, S, SSD_GROUPS, SSD_WIDTH // SSD_GROUPS),
                     ssd_norm_w[l].reshape(SSD_GROUPS, SSD_WIDTH // SSD_GROUPS)).reshape(B, S, SSD_WIDTH)

        y = jnp.concatenate([ya, yb, yc], axis=-1) @ w_out[l]
        x = x + gate[:, None] * y

    return rmsnorm(x, final_norm_w)
```

```python
import numpy as np
from contextlib import ExitStack
import concourse.bass as bass
import concourse.mybir as mybir
from concourse.bass_utils import run_bass_kernel_spmd

F32 = mybir.dt.float32
BF16 = mybir.dt.bfloat16
U32 = mybir.dt.uint32
AF = mybir.ActivationFunctionType
ALU = mybir.AluOpType

D = 1024
S = 2048
DEPTH = 4
NIN = 5648
T = 512
NT = T // 128
EPS = 1e-6
LC = 132
NCOLS = LC * DEPTH + 16
ND = 52
NSLOT = 40
OFF = dict(ax=0, ag=512, q=1024, f=1536, i=2048, g=2560, z=3072, xs=4096, B=5120, C=5376, dt=5632)


class Prog:
    def __init__(self, nc, st):
        self.nc = nc
        self.st = st
        self.eng = {"pe": nc.tensor, "act": nc.scalar, "dve": nc.vector, "pool": nc.gpsimd, "sp": nc.sync}
        self.sem = {e: st.enter_context(nc.semaphore("sem_" + e)) for e in ("pe", "act", "dve", "pool")}
        self.cnt = {e: 0 for e in self.sem}
        self.NDS = 12
        self.dsem = [st.enter_context(nc.semaphore("dsem%d" % i)) for i in range(self.NDS)]
        self.dval = [0] * self.NDS
        self.dnext = 0
        self.water = {e: {} for e in self.eng}
        self.lastw = {}
        self.readers = {}
        self.bank_i = 0

    def _semh(self, s):
        return self.sem[s] if isinstance(s, str) else self.dsem[s[1]]

    def _wait(self, e, deps):
        need = {}
        for (s, v) in deps:
            if s == e and e == "pe":
                continue
            if v > need.get(s, 0):
                need[s] = v
        for s, v in need.items():
            if self.water[e].get(s, 0) >= v:
                continue
            self.eng[e].wait_ge(self._semh(s), v)
            self.water[e][s] = v

    def _deps(self, r, w):
        deps = set()
        for k in r:
            t = self.lastw.get(k)
            if t:
                deps.add(t)
        for k in w:
            t = self.lastw.get(k)
            if t:
                deps.add(t)
            for t in self.readers.get(k, ()):
                deps.add(t)
        return deps

    def _commit(self, tok, r, w):
        for k in r:
            self.readers.setdefault(k, []).append(tok)
        for k in w:
            self.lastw[k] = tok
            self.readers[k] = []

    def op(self, e, fn, r=(), w=()):
        r = list(r)
        w = list(w)
        self._wait(e, self._deps(r, w))
        ins = fn(self.eng[e])
        self.cnt[e] += 1
        ins.then_inc(self.sem[e], 1)
        self._commit((e, self.cnt[e]), r, w)

    def dma(self, q, out, in_, r=(), w=()):
        r = list(r)
        w = list(w)
        i = self.dnext
        self.dnext = (self.dnext + 1) % self.NDS
        deps = self._deps(r, w)
        if self.dval[i] > 0:
            deps.add((("d", i), self.dval[i]))
        self._wait(q, deps)
        self.dval[i] += 16
        self.eng[q].dma_start(out=out, in_=in_).then_inc(self.dsem[i], 16)
        self._commit((("d", i), self.dval[i]), r, w)

    def wait_all(self, e, keys):
        deps = set()
        for k in keys:
            t = self.lastw.get(k)
            if t:
                deps.add(t)
        self._wait(e, deps)


def _build(nseg=S // T, nlayers=DEPTH, dbg=False):
    nc = bass.Bass("TRN2", target_bir_lowering=False)
    x_d = nc.dram_tensor("x", [S, D], F32, kind="ExternalInput").ap()
    wada_d = nc.dram_tensor("w_ada", [DEPTH, D, 3 * D], F32, kind="ExternalInput").ap()
    win_d = nc.dram_tensor("w_in", [DEPTH, D, NIN], F32, kind="ExternalInput").ap()
    wout_d = nc.dram_tensor("w_out", [DEPTH, 2 * D, D], F32, kind="ExternalInput").ap()
    cols_d = nc.dram_tensor("colsF", [128, NCOLS], F32, kind="ExternalInput").ap()
    rowp_d = nc.dram_tensor("rowp", [DEPTH, 1072], F32, kind="ExternalInput").ap()
    lrubd_d = nc.dram_tensor("lrubd", [DEPTH, 128, 1024], F32, kind="ExternalInput").ap()
    cf_d = nc.dram_tensor("constf", [128, 384], F32, kind="ExternalInput").ap()
    cm_d = nc.dram_tensor("constm", [128, 128], U32, kind="ExternalInput").ap()
    out_d = nc.dram_tensor("out", [S, D], F32, kind="ExternalOutput").ap()
    if dbg:
        dbg_y = nc.dram_tensor("dbg_y", [128, 16 * T], BF16, kind="ExternalOutput").ap()
        dbg_x = nc.dram_tensor("dbg_x", [128, 8 * T], F32, kind="ExternalOutput").ap()

    with ExitStack() as st:
        P = Prog(nc, st)

        def sb(name, shape, dt):
            return st.enter_context(nc.sbuf_tensor(name, shape, dt))

        xT = sb("xT", [128, 8, T], F32)
        hT = sb("hT", [128, 8, T], BF16)
        yT = sb("yT", [128, 16, T], BF16)
        NW = 8
        wbf = [sb("wbf%d" % i, [128, 2048], BF16) for i in range(NW)]
        colsF = sb("colsF_sb", [128, NCOLS], F32)
        dcol = sb("dcol", [128, DEPTH, ND], F32)
        lrub = sb("lrub", [128, 1024], BF16)
        cf = sb("cf", [128, 384], F32)
        identb = sb("identb", [128, 128], BF16)
        onesb = sb("onesb", [128, 128], BF16)
        maskbd = sb("maskbd", [128, 128], U32)
        rmask = sb("rmask", [128, T], F32)
        rowb = sb("rowb", [128, 1072], F32)
        abc = sb("abc", [128, 16], F32)
        nrm = sb("nrm", [128, T], F32)
        cbuf = [sb("cbuf%d" % i, [128, T + 3], F32) for i in range(2)]
        PT8 = [sb("PT%d" % i, [128, 128], BF16) for i in range(8)]
        khtm4 = [sb("khtm%d" % i, [128, 4, 128], BF16) for i in range(4)]
        smalls = sb("smalls", [128, 560], F32)
        arena = sb("arena", [128, NSLOT * 512], F32)
        arena_b = arena[:].bitcast(BF16)
        lru_tail = sb("lru_tail", [128, DEPTH, 12], F32)
        lru_h = sb("lru_h", [128, DEPTH, 4], F32)
        ssd_tail = sb("ssd_tail", [128, DEPTH, 36], F32)
        hgS = sb("hgS", [128, DEPTH, 512], F32)
        ssS = sb("ssS", [128, DEPTH, 1024], F32)
        hgSb = sb("hgSb", [128, 512], BF16)
        ssSb = sb("ssSb", [128, 1024], BF16)
        psum = [st.enter_context(nc.psum_tensor("ps%d" % i, [128, 512], F32)) for i in range(8)]

        ident = cf[:, 0:128]
        tri = cf[:, 128:256]
        onesf = cf[:, 256:384]

        def bank():
            i = P.bank_i
            P.bank_i = (P.bank_i + 1) % 6
            return psum[i], [("ps", i)]

        def AR(i, n=1):
            return arena[:, i * 512:(i + n) * 512], [("ar", k) for k in range(i, i + n)]

        def ARb(i, n=1):
            return arena_b[:, i * 1024:(i + n) * 1024], [("ar", k) for k in range(i, i + n)]

        wctr = [0, 0]

        WPOOL = {"m": (0, 5), "c": (5, 3)}
        wpc = {"m": 0, "c": 0}

        def load_w(src, nfree, shape3, pool="m"):
            base, n = WPOOL[pool]
            wi = base + wpc[pool] % n
            wpc[pool] += 1
            a, b = shape3
            wv = wbf[wi][:, 0:nfree].rearrange("p (a b) -> p a b", a=a)
            P.dma("pool", wv, src, r=[], w=[("wbf", wi)])
            return wv, [("wbf", wi)]

        def load_win(l, c0, ncol=256, pool="m"):
            src = win_d[l].rearrange("(k p) n -> p k n", p=128)[:, :, c0:c0 + ncol]
            return load_w(src, 8 * ncol, (8, ncol), pool)

        def load_wout(l, dm):
            src = wout_d[l].rearrange("(k p) n -> p k n", p=128)[:, :, dm * 128:(dm + 1) * 128]
            return load_w(src, 16 * 128, (16, 128))

        hkeys = [("hT", c) for c in range(8)]

        def fm_mm(ps, pk, wv, wk, col0):
            for k in range(8):
                P.op("pe", lambda e, k=k: e.matmul(ps[:, 0:T], lhsT=wv[:, k, col0:col0 + 128], rhs=hT[:, k, :],
                                                   start=(k == 0), stop=(k == 7)),
                     r=wk + [("hT", k)], w=pk)

        def tm_mm(ps, pk, wv, wk, tt, c0, ncol, o0):
            for k in range(8):
                P.op("pe", lambda e, k=k: e.matmul(ps[:, o0:o0 + ncol], lhsT=hT[:, k, tt * 128:(tt + 1) * 128],
                                                   rhs=wv[:, k, c0:c0 + ncol], start=(k == 0), stop=(k == 7)),
                     r=wk + [("hT", k)], w=pk)

        P.dma("sp", colsF[:], cols_d[:, :], w=["colsF"])
        P.dma("sp", cf[:], cf_d[:, :], w=["cf"])
        P.dma("sp", maskbd[:], cm_d[:, :], w=["maskbd"])
        P.op("dve", lambda e: e.tensor_copy(out=identb[:], in_=ident), r=["cf"], w=["identb"])
        P.op("dve", lambda e: e.tensor_copy(out=onesb[:], in_=onesf), r=["cf"], w=["onesb"])
        P.op("dve", lambda e: e.memset(rmask[:], 1.0), w=["rmask"])
        P.op("dve", lambda e: e.memset(rmask[:].rearrange("p (c j) -> p c j", j=64)[:, :, 0:1], 0.0), w=["rmask"])
        for i in range(8):
            P.op("dve", lambda e, i=i: e.memset(PT8[i][:], 0.0), w=[("PT", i)])
        P.op("pool", lambda e: e.memset(lru_tail[:], 0.0), w=["lru_tail"])
        P.op("pool", lambda e: e.memset(lru_h[:], 0.0), w=["lru_h"])
        P.op("pool", lambda e: e.memset(ssd_tail[:], 0.0), w=["ssd_tail"])
        P.op("pool", lambda e: e.memset(hgS[:], 0.0), w=[("hgS", i) for i in range(4)])
        P.op("pool", lambda e: e.memset(ssS[:], 0.0), w=[("ssS", i) for i in range(2)])

        CB = LC * DEPTH
        sm = smalls
        P.op("act", lambda e: e.activation(out=sm[:, 0:8], in_=colsF[:, CB + 8:CB + 16], func=AF.Silu),
             r=["colsF"], w=["sm_cond"])
        for l in range(DEPTH):
            P.op("act", lambda e, l=l: e.activation(out=sm[:, 16 + 4 * l:20 + 4 * l],
                                                    in_=colsF[:, l * LC + 64:l * LC + 68], func=AF.Exp),
                 r=["colsF"], w=["sm_ex"])
        P.op("dve", lambda e: e.tensor_tensor(out=sm[:, 32:36], in0=sm[:, 16:20], in1=sm[:, 20:24], op=ALU.add),
             r=["sm_ex"], w=["sm_tot"])
        P.op("dve", lambda e: e.tensor_tensor(out=sm[:, 32:36], in0=sm[:, 32:36], in1=sm[:, 24:28], op=ALU.add),
             r=["sm_ex", "sm_tot"], w=["sm_tot"])
        P.op("dve", lambda e: e.tensor_tensor(out=sm[:, 32:36], in0=sm[:, 32:36], in1=sm[:, 28:32], op=ALU.add),
             r=["sm_ex", "sm_tot"], w=["sm_tot"])
        P.op("dve", lambda e: e.reciprocal(out=sm[:, 36:40], in_=sm[:, 32:36]), r=["sm_tot"], w=["sm_rt"])
        P.op("dve", lambda e: e.memset(dcol[:, 0, 40:44], 0.0), w=[("dcol", 0)])
        for l in range(1, DEPTH):
            P.op("dve", lambda e, l=l: e.tensor_tensor(out=sm[:, 40:44], in0=sm[:, 16 + 4 * l:20 + 4 * l],
                                                       in1=sm[:, 36:40], op=ALU.mult),
                 r=["sm_ex", "sm_rt"], w=["sm_p"])
            P.op("dve", lambda e, l=l: e.tensor_tensor(out=dcol[:, l, 40:44], in0=dcol[:, l - 1, 40:44],
                                                       in1=sm[:, 40:44], op=ALU.add),
                 r=["sm_p", ("dcol", l - 1)], w=[("dcol", l)])
        for l in range(DEPTH):
            dk = [("dcol", l)]
            P.op("dve", lambda e, l=l: e.tensor_scalar(out=dcol[:, l, 44:48], in0=dcol[:, l, 40:44], scalar1=-0.5,
                                                       scalar2=0.5, op0=ALU.mult, op1=ALU.add), r=dk, w=dk)
            P.op("dve", lambda e, l=l: e.tensor_scalar(out=dcol[:, l, 48:52], in0=dcol[:, l, 40:44], scalar1=0.5,
                                                       scalar2=0.5, op0=ALU.mult, op1=ALU.add), r=dk, w=dk)
            P.op("dve", lambda e, l=l: e.tensor_scalar(out=dcol[:, l, 24:32], in0=colsF[:, l * LC + 52:l * LC + 60],
                                                       scalar1=0.5, scalar2=None, op0=ALU.mult), r=dk + ["colsF"], w=dk)
            P.op("act", lambda e, l=l: e.activation(out=sm[:, 48:52], in_=colsF[:, l * LC + 60:l * LC + 64],
                                                    func=AF.Exp, scale=-1.0), r=["colsF"], w=["sm_sp"])
            P.op("act", lambda e: e.activation(out=sm[:, 48:52], in_=sm[:, 48:52], func=AF.Ln, bias=1.0),
                 r=["sm_sp"], w=["sm_sp"])
            P.op("dve", lambda e, l=l: e.tensor_scalar(out=dcol[:, l, 32:36], in0=sm[:, 48:52], scalar1=-4.0,
                                                       scalar2=None, op0=ALU.mult), r=dk + ["sm_sp"], w=dk)
            P.op("dve", lambda e, l=l: e.tensor_scalar(out=dcol[:, l, 36:40], in0=sm[:, 48:52], scalar1=-8.0,
                                                       scalar2=None, op0=ALU.mult), r=dk + ["sm_sp"], w=dk)

        modrow, mrk = AR(0, 6)
        one11 = onesf[0:1, 0:1]
        for l in range(nlayers):
            for s in range(12):
                si = wctr[0] % 2
                wctr[0] += 1
                src = wada_d[l].rearrange("(k p) n -> p k n", p=128)[:, :, s * 256:(s + 1) * 256]
                stv, stk = AR(8 + 4 * si, 4)
                sv = stv.rearrange("p (a b) -> p a b", a=8)
                P.dma("sp", sv, src, w=stk)
                ps, pk = bank()
                for k in range(8):
                    P.op("pe", lambda e, k=k, sv=sv, ps=ps: e.matmul(ps[0:1, 0:256], lhsT=sm[:, k:k + 1], rhs=sv[:, k, :],
                                                                      start=(k == 0), stop=(k == 7)),
                         r=stk + ["sm_cond"], w=pk)
                P.op("act", lambda e, s=s, ps=ps: e.activation(out=modrow[0:1, s * 256:(s + 1) * 256], in_=ps[0:1, 0:256],
                                                               func=AF.Copy), r=pk, w=mrk)
            ps, pk = bank()
            for c in range(24):
                P.op("pe", lambda e, c=c, ps=ps: e.matmul(ps[:, c:c + 1], lhsT=modrow[0:1, c * 128:(c + 1) * 128], rhs=one11,
                                                          start=True, stop=True), r=mrk + ["cf"], w=pk)
            dk = [("dcol", l)]
            P.op("dve", lambda e, l=l, ps=ps: e.tensor_tensor(out=dcol[:, l, 8:16], in0=ps[:, 0:8],
                                                              in1=colsF[:, l * LC + 8:l * LC + 16], op=ALU.add),
                 r=pk + ["colsF"], w=dk)
            P.op("dve", lambda e, l=l, ps=ps: e.tensor_tensor(out=dcol[:, l, 16:24], in0=ps[:, 16:24],
                                                              in1=colsF[:, l * LC + 24:l * LC + 32], op=ALU.add),
                 r=pk + ["colsF"], w=dk)
            P.op("dve", lambda e, l=l, ps=ps: e.tensor_tensor(out=sm[:, 56:64], in0=ps[:, 8:16],
                                                              in1=colsF[:, l * LC + 16:l * LC + 24], op=ALU.add),
                 r=pk + ["colsF"], w=["sm_sc"])
            P.op("dve", lambda e, l=l: e.scalar_tensor_tensor(out=dcol[:, l, 0:8], in0=sm[:, 56:64], scalar=1.0,
                                                              in1=colsF[:, l * LC:l * LC + 8], op0=ALU.add, op1=ALU.mult),
                 r=["sm_sc", "colsF"], w=dk)

        xkeys = [("xT", c) for c in range(8)]
        ykeys = [("yT", c) for c in range(16)]

        def rms_bcast(src_keys, nfeat):
            pass

        def load_x_dma(j):
            for tt in range(NT):
                xin, xk = AR(16 + tt * 2, 2)
                P.dma("sp", xin, x_d[j * T + tt * 128:j * T + (tt + 1) * 128, :], w=xk)

        def load_x_tr(j):
            for c in range(8):
                ps, pk = bank()
                for tt in range(NT):
                    xin, xk = AR(16 + tt * 2, 2)
                    P.op("pe", lambda e, tt=tt, xin=xin, ps=ps: e.transpose(ps[:, tt * 128:(tt + 1) * 128],
                                                                            xin[:, c * 128:(c + 1) * 128], ident),
                         r=xk + ["cf"], w=pk)
                P.op("act" if c % 2 else "dve",
                     (lambda e, ps=ps, c=c: e.activation(out=xT[:, c, :], in_=ps[:, 0:T], func=AF.Copy)) if c % 2 else
                     (lambda e, ps=ps, c=c: e.tensor_copy(out=xT[:, c, :], in_=ps[:, 0:T])),
                     r=pk, w=[("xT", c)])

        def norm_phase(gcol, shcol, dst_bf):
            for hf_ in range(2):
                P.op("act", lambda e, hf_=hf_: e.activation(out=hT[:, hf_ * 4:(hf_ + 1) * 4, :].rearrange("p a b -> p (a b)"),
                                                            in_=xT[:, hf_ * 4:(hf_ + 1) * 4, :].rearrange("p a b -> p (a b)"),
                                                            func=AF.Square),
                     r=xkeys[hf_ * 4:(hf_ + 1) * 4], w=hkeys[hf_ * 4:(hf_ + 1) * 4])
            ps, pk = bank()
            for c in range(8):
                P.op("pe", lambda e, c=c: e.matmul(ps[:, 0:T], lhsT=onesb[:], rhs=hT[:, c, :], start=(c == 0), stop=(c == 7)),
                     r=[("hT", c), "onesb"], w=pk)
            P.op("act", lambda e: e.activation(out=nrm[:], in_=ps[:, 0:T], func=AF.Ln, scale=1.0 / D, bias=EPS),
                 r=pk, w=["nrm"])
            P.op("act", lambda e: e.activation(out=nrm[:], in_=nrm[:], func=AF.Exp, scale=-0.5), r=["nrm"], w=["nrm"])
            for c in range(8):
                if dst_bf:
                    tmp, tk = AR(26 + (c % 2))
                    P.op("dve", lambda e, c=c, tmp=tmp: e.tensor_tensor(out=tmp, in0=xT[:, c, :], in1=nrm[:], op=ALU.mult),
                         r=[("xT", c), "nrm"], w=tk)
                    P.op("act", lambda e, c=c, tmp=tmp: e.activation(out=hT[:, c, :], in_=tmp, func=AF.Identity,
                                                                     scale=gcol[:, c:c + 1], bias=shcol[:, c:c + 1]),
                         r=tk + ["dcol_all"], w=[("hT", c)])
                else:
                    tmp, tk = AR(c)
                    P.op("dve", lambda e, c=c, tmp=tmp: e.tensor_tensor(out=tmp, in0=xT[:, c, :], in1=nrm[:], op=ALU.mult),
                         r=[("xT", c), "nrm"], w=tk)
                    P.op("act", lambda e, c=c, tmp=tmp: e.activation(out=tmp, in_=tmp, func=AF.Copy, scale=gcol[:, c:c + 1]),
                         r=tk + ["colsF"], w=tk)

        def conv_chunk(ps, pk, tail_ap, tail_key, wcol, dst, dk, ci):
            cb = cbuf[ci % 2]
            ck = [("cbuf", ci % 2)]
            P.op("act", lambda e: e.activation(out=cb[:, 3:3 + T], in_=ps[:, 0:T], func=AF.Copy), r=pk, w=ck)
            P.op("act", lambda e: e.activation(out=dst, in_=ps[:, 0:T], func=AF.Copy, scale=wcol[:, 3:4]),
                 r=pk + ["colsF"], w=dk)
            P.op("dve", lambda e: e.tensor_copy(out=cb[:, 0:3], in_=tail_ap), r=[tail_key], w=ck)
            P.op("dve", lambda e: e.tensor_copy(out=tail_ap, in_=cb[:, T:T + 3]), r=ck, w=[tail_key])
            for k in range(3):
                P.op("dve", lambda e, k=k: e.scalar_tensor_tensor(out=dst, in0=cb[:, k:k + T], scalar=wcol[:, k:k + 1],
                                                                  in1=dst, op0=ALU.mult, op1=ALU.add),
                     r=ck + dk + ["colsF"], w=dk)

        def unit(j, l):
            cb0 = l * LC
            dc = dcol[:, l, :]
            P.dma("sp", rowb[:], rowp_d[l:l + 1, :].partition_broadcast(128), w=["rowb"])
            P.dma("pool", lrub[:], lrubd_d[l], w=["lrub"])
            P.op("act", lambda e: e.activation(out=hgSb[:], in_=hgS[:, l, :], func=AF.Copy),
                 r=[("hgS", i) for i in range(4)], w=[("hgSb", i) for i in range(4)])
            P.op("act", lambda e: e.activation(out=ssSb[:], in_=ssS[:, l, :], func=AF.Copy),
                 r=[("ssS", i) for i in range(2)], w=[("ssSb", i) for i in range(2)])

            norm_phase(dcol[:, l, 0:8], dcol[:, l, 8:16], True)

            P.op("act", lambda e: e.activation(out=abc[:], in_=rowb[:, 1040:1056], func=AF.Exp), r=["rowb"], w=["abc"])
            P.op("dve", lambda e: e.tensor_scalar(out=abc[:], in0=abc[:], scalar1=-1.0, scalar2=None, op0=ALU.mult),
                 r=["abc"], w=["abc"])

            sztm, szk = ARb(28, 4)
            xsT, xsTk = ARb(32, 4)
            bcT, bcTk = ARb(36, 2)
            dtt = sm[:, 80:144]
            dAt = sm[:, 144:208]

            def gen_cpre():
                for hf_ in range(2):
                    wz = [load_win(l, OFF["z"] + 256 * (hf_ * 2 + i), pool="c") for i in range(2)]
                    for tt in range(NT):
                        ps, pk = bank()
                        for q4 in range(2):
                            tm_mm(ps, pk, wz[q4][0], wz[q4][1], tt, 0, 256, q4 * 256)
                        P.op("act", lambda e, ps=ps, tt=tt, hf_=hf_: e.activation(
                            out=sztm[:, tt * 1024 + hf_ * 512:tt * 1024 + (hf_ + 1) * 512], in_=ps[:, 0:512], func=AF.Silu),
                            r=pk, w=szk)
                        yield
                pend_c = [None]
                for cc in range(12):
                    if cc % 2 == 0:
                        wv, wk = load_win(l, OFF["xs"] + 128 * cc, pool="c")
                    ps, pk = bank()
                    fm_mm(ps, pk, wv, wk, (cc % 2) * 128)
                    tail = ssd_tail[:, l, cc * 3:cc * 3 + 3]
                    xc, xck = AR(38 + (cc % 2))
                    wcol = colsF[:, cb0 + 72 + cc * 4:cb0 + 76 + cc * 4]
                    conv_chunk(ps, pk, tail, "ssd_tail", wcol, xc, xck, cc)
                    if cc < 8:
                        dst = xsT[:, cc * 512:(cc + 1) * 512]
                        dk_ = xsTk
                    else:
                        dst = bcT[:, (cc - 8) * 512:(cc - 7) * 512]
                        dk_ = bcTk

                    def fin_c(dst=dst, xc=xc, cc=cc, xck=xck, dk_=dk_):
                        P.op("act", lambda e: e.activation(out=dst, in_=xc, func=AF.Silu,
                                                           bias=colsF[:, cb0 + 120 + cc:cb0 + 121 + cc]),
                             r=xck + ["colsF"], w=dk_)
                    if pend_c[0] is not None:
                        pend_c[0]()
                    pend_c[0] = fin_c
                    yield
                pend_c[0]()

            def gen_dt():
                wdt = load_win(l, OFF["dt"], 16, pool="c")
                for tt in range(NT):
                    ps, pk = bank()
                    tm_mm(ps, pk, wdt[0], wdt[1], tt, 0, 16, 0)
                    P.op("dve", lambda e, ps=ps, tt=tt: e.tensor_tensor(out=dtt[:, tt * 16:(tt + 1) * 16], in0=ps[:, 0:16],
                                                                        in1=rowb[:, 1024:1040], op=ALU.add),
                         r=pk + ["rowb"], w=["sm_dt"])
                    yield
                P.op("act", lambda e: e.activation(out=dtt, in_=dtt, func=AF.Exp), r=["sm_dt"], w=["sm_dt"])
                P.op("act", lambda e: e.activation(out=dtt, in_=dtt, func=AF.Ln, bias=1.0), r=["sm_dt"], w=["sm_dt"])
                P.op("dve", lambda e: e.tensor_tensor(out=dAt.rearrange("p (a b) -> p a b", b=16),
                                                      in0=dtt.rearrange("p (a b) -> p a b", b=16),
                                                      in1=abc[:].unsqueeze(1).to_broadcast([128, 4, 16]), op=ALU.mult),
                     r=["sm_dt", "abc"], w=["sm_dA"])

            cgen = gen_cpre()
            dgen = gen_dt()

            def step_c():
                next(cgen, None)

            def step_dt():
                next(dgen, None)

            wx = [load_win(l, OFF["ax"] + 256 * i) for i in range(2)]
            pend_a = [None]
            for jc in range(4):
                ps, pk = bank()
                fm_mm(ps, pk, wx[jc // 2][0], wx[jc // 2][1], (jc % 2) * 128)
                tail = lru_tail[:, l, jc * 3:jc * 3 + 3]
                xa, xak = AR(jc)
                wcol = colsF[:, cb0 + 32 + jc * 4:cb0 + 36 + jc * 4]
                conv_chunk(ps, pk, tail, "lru_tail", wcol, xa, xak, jc)

                def fin_a(xa=xa, xak=xak, jc=jc):
                    xab, xabk = ARb(4 + jc)
                    P.op("act", lambda e: e.activation(out=xab[:, 0:T], in_=xa, func=AF.Identity,
                                                       bias=colsF[:, cb0 + 48 + jc:cb0 + 49 + jc]),
                         r=xak + ["colsF"], w=xabk)
                    P.op("act", lambda e: e.activation(out=xa, in_=xa, func=AF.Identity,
                                                       bias=colsF[:, cb0 + 48 + jc:cb0 + 49 + jc]),
                         r=xak + ["colsF"], w=xak)
                if pend_a[0] is not None:
                    pend_a[0]()
                pend_a[0] = fin_a
                step_c()
            pend_a[0]()
            wg = [load_win(l, OFF["ag"] + 256 * i) for i in range(2)]
            for jc in range(4):
                xab, xabk = ARb(4 + jc)
                rr, rk = AR(8 + jc)
                ii, ik = AR(12 + jc)
                sg, sgk = AR(20 + jc)
                ps, pk = bank()
                P.op("pe", lambda e, ps=ps, jc=jc, xab=xab: e.matmul(ps[:, 0:T], lhsT=lrub[:, jc * 128:(jc + 1) * 128],
                                                                     rhs=xab[:, 0:T], start=True, stop=True),
                     r=xabk + ["lrub"], w=pk)
                P.op("act", lambda e, ps=ps, jc=jc, rr=rr: e.activation(out=rr, in_=ps[:, 0:T], func=AF.Tanh, scale=0.5,
                                                                        bias=dc[:, 24 + jc:25 + jc]),
                     r=pk + ["dcol_all"], w=rk)
                ps, pk = bank()
                P.op("pe", lambda e, ps=ps, jc=jc, xab=xab: e.matmul(ps[:, 0:T], lhsT=lrub[:, (4 + jc) * 128:(5 + jc) * 128],
                                                                     rhs=xab[:, 0:T], start=True, stop=True),
                     r=xabk + ["lrub"], w=pk)
                P.op("act", lambda e, ps=ps, jc=jc, ii=ii: e.activation(out=ii, in_=ps[:, 0:T], func=AF.Tanh, scale=0.5,
                                                                        bias=dc[:, 28 + jc:29 + jc]),
                     r=pk + ["dcol_all"], w=ik)
                ps, pk = bank()
                fm_mm(ps, pk, wg[jc // 2][0], wg[jc // 2][1], (jc % 2) * 128)
                P.op("act", lambda e, ps=ps, sg=sg: e.activation(out=sg, in_=ps[:, 0:T], func=AF.Silu), r=pk, w=sgk)
                step_c()
            for jc in range(4):
                xa, xak = AR(jc)
                rr, rk = AR(8 + jc)
                ii, ik = AR(12 + jc)
                aa, ak = AR(16 + jc)
                P.op("dve", lambda e, ii=ii, xa=xa: e.scalar_tensor_tensor(out=ii, in0=ii, scalar=1.0, in1=xa, op0=ALU.add,
                                                                          op1=ALU.mult), r=ik + xak, w=ik)
                P.op("act", lambda e, aa=aa, rr=rr, jc=jc: e.activation(out=aa, in_=rr, func=AF.Exp, scale=dc[:, 32 + jc:33 + jc],
                                                                        bias=dc[:, 32 + jc:33 + jc]), r=rk + ["dcol_all"], w=ak)
                P.op("act", lambda e, rr=rr, jc=jc: e.activation(out=rr, in_=rr, func=AF.Exp, scale=dc[:, 36 + jc:37 + jc],
                                                                 bias=dc[:, 36 + jc:37 + jc]), r=rk + ["dcol_all"], w=rk)
                P.op("act", lambda e, rr=rr: e.activation(out=rr, in_=rr, func=AF.Ln, scale=-1.0, bias=1.0), r=rk, w=rk)
                P.op("act", lambda e, rr=rr: e.activation(out=rr, in_=rr, func=AF.Exp, scale=0.5, bias=float(np.log(0.5))),
                     r=rk, w=rk)
            for jc in range(4):
                xa, xak = AR(jc)
                rr, rk = AR(8 + jc)
                ii, ik = AR(12 + jc)
                aa, ak = AR(16 + jc)
                sg, sgk = AR(20 + jc)
                P.op("dve", lambda e, ii=ii, rr=rr: e.tensor_tensor(out=ii, in0=ii, in1=rr, op=ALU.mult), r=ik + rk, w=ik)
                P.op("dve", lambda e, xa=xa, aa=aa, ii=ii, jc=jc: e.tensor_tensor_scan(out=xa, data0=aa, data1=ii,
                                                                                     initial=lru_h[:, l, jc:jc + 1],
                                                                                     op0=ALU.mult, op1=ALU.add),
                     r=ak + ik + ["lru_h"], w=xak)
                P.op("dve", lambda e, xa=xa, jc=jc: e.tensor_copy(out=lru_h[:, l, jc:jc + 1], in_=xa[:, T - 1:T]),
                     r=xak, w=["lru_h"])
                P.op("dve", lambda e, xa=xa, sg=sg, jc=jc: e.tensor_tensor(out=yT[:, jc, :], in0=xa, in1=sg, op=ALU.mult),
                     r=xak + sgk, w=[("yT", jc)])

            wgg = [load_win(l, OFF["g"] + 256 * i) for i in range(2)]
            vtm, vtk = ARb(8, 2)
            for jh in range(4):
                ps, pk = bank()
                fm_mm(ps, pk, wgg[jh // 2][0], wgg[jh // 2][1], (jh % 2) * 128)
                sgBb, sgBk = ARb(jh // 2)
                sgB = sgBb[:, (jh % 2) * 512:(jh % 2) * 512 + T]
                P.op("act", lambda e, ps=ps, sgB=sgB: e.activation(out=sgB, in_=ps[:, 0:T], func=AF.Silu), r=pk, w=sgBk)
                step_c()
            wi = [load_win(l, OFF["i"] + 256 * i) for i in range(2)]
            for tt in range(NT):
                ps, pk = bank()
                for hf_ in range(2):
                    tm_mm(ps, pk, wi[hf_][0], wi[hf_][1], tt, 0, 256, hf_ * 256)
                P.op("act", lambda e, ps=ps, tt=tt: e.activation(out=vtm[:, tt * 512:(tt + 1) * 512], in_=ps[:, 0:512],
                                                                 func=AF.Copy), r=pk, w=vtk)
                step_c()
            wq = [load_win(l, OFF["q"] + 256 * i) for i in range(2)]
            wf = [load_win(l, OFF["f"] + 256 * i) for i in range(2)]
            for jh in range(4):
                qq, qk = AR(10 + jh)
                fv, fk = AR(14 + jh)
                ps, pk = bank()
                fm_mm(ps, pk, wq[jh // 2][0], wq[jh // 2][1], (jh % 2) * 128)
                P.op("act", lambda e, ps=ps, qq=qq: e.activation(out=qq, in_=ps[:, 0:T], func=AF.Silu), r=pk, w=qk)
                ps, pk = bank()
                fm_mm(ps, pk, wf[jh // 2][0], wf[jh // 2][1], (jh % 2) * 128)
                P.op("act", lambda e, ps=ps, fv=fv: e.activation(out=fv, in_=ps[:, 0:T], func=AF.Tanh, scale=0.5), r=pk, w=fk)
                P.op("act", lambda e, fv=fv, jh=jh: e.activation(out=fv, in_=fv, func=AF.Identity, scale=dc[:, 44 + jh:45 + jh],
                                                                 bias=dc[:, 48 + jh:49 + jh]),
                     r=fk + ["dcol_all"], w=fk)
                step_c()
            for _ in cgen:
                pass
            HB = {}
            TSET = [
                [AR(18), AR(19), AR(20), AR(21), AR(22), AR(23)],
                [AR(2), AR(3), AR(38), AR(39), (nrm[:], ["nrm"]), (cbuf[0][:, 0:T], [("cbuf", 0)])],
            ]

            def gen_prep(jh, ts):
                qq, qk = AR(10 + jh)
                fv, fk = AR(14 + jh)
                (lf, lfk), (kk, kkk), (cum, cumk), (d1, d1k), (ek, ekk), (ec, eck) = TSET[ts]
                qkt, qktk = ARb(14 + jh)
                qhk, qhkk = ARb(24 + jh)
                qt = qkt[:, 0:T]
                kt = qkt[:, T:2 * T]
                qh = qhk[:, 0:T]
                khF = qhk[:, T:2 * T]
                ex8 = sm[:, 264 + 8 * jh:272 + 8 * jh]
                ex8k = ["sm_ex8_%d" % jh]
                kht = khtm4[jh]
                khtk = [("khtm", jh)]
                HB[jh] = dict(qt=qt, kt=kt, qh=qh, qktk=qktk, qhkk=qhkk, ex8=ex8, ex8k=ex8k, kht=kht, khtk=khtk)
                P.op("act", lambda e: e.activation(out=lf, in_=fv, func=AF.Ln), r=fk, w=lfk)
                P.op("act", lambda e: e.activation(out=kk, in_=fv, func=AF.Copy, scale=-1.0, bias=1.0), r=fk, w=kkk)
                yield
                P.op("dve", lambda e: e.tensor_tensor_scan(out=cum, data0=rmask[:], data1=lf, initial=0.0,
                                                           op0=ALU.mult, op1=ALU.add), r=lfk + ["rmask"], w=cumk)
                c3 = cum.rearrange("p (c j) -> p c j", j=64)
                P.op("dve", lambda e: e.tensor_tensor(out=d1.rearrange("p (c j) -> p c j", j=64), in0=c3,
                                                      in1=c3[:, :, 31:32].to_broadcast([128, 8, 64]),
                                                      op=ALU.subtract), r=cumk, w=d1k)
                P.op("dve", lambda e: e.tensor_tensor(out=lf.rearrange("p (c j) -> p c j", j=64), in0=c3,
                                                      in1=c3[:, :, 63:64].to_broadcast([128, 8, 64]),
                                                      op=ALU.subtract), r=cumk, w=lfk)
                yield
                P.op("act", lambda e: e.activation(out=ek, in_=d1, func=AF.Exp, scale=-1.0), r=d1k, w=ekk)
                P.op("act", lambda e: e.activation(out=d1, in_=d1, func=AF.Exp), r=d1k, w=d1k)
                P.op("act", lambda e: e.activation(out=lf, in_=lf, func=AF.Exp, scale=-1.0), r=lfk, w=lfk)
                P.op("act", lambda e: e.activation(out=ec, in_=cum, func=AF.Exp), r=cumk, w=eck)
                P.op("act", lambda e: e.activation(out=ex8, in_=c3[:, :, 63], func=AF.Exp), r=cumk, w=ex8k)
                yield
                P.op("dve", lambda e: e.tensor_tensor(out=kt, in0=kk, in1=ek, op=ALU.mult), r=kkk + ekk, w=qktk)
                P.op("dve", lambda e: e.tensor_tensor(out=qt, in0=qq, in1=d1, op=ALU.mult), r=qk + d1k, w=qktk)
                P.op("dve", lambda e: e.tensor_tensor(out=khF, in0=kk, in1=lf, op=ALU.mult), r=kkk + lfk, w=qhkk)
                P.op("dve", lambda e: e.tensor_tensor(out=qh, in0=qq, in1=ec, op=ALU.mult), r=qk + eck, w=qhkk)
                yield
                ps, pk = bank()
                psb = ps[:].bitcast(BF16)
                for tt in range(NT):
                    P.op("pe", lambda e, tt=tt: e.transpose(psb[:, tt * 128:(tt + 1) * 128],
                                                            khF[:, tt * 128:(tt + 1) * 128], identb[:]),
                         r=qhkk + ["identb"], w=pk)
                P.op("act", lambda e: e.activation(out=kht[:].rearrange("p a b -> p (a b)"), in_=psb[:, 0:512],
                                                   func=AF.Copy), r=pk, w=khtk)

            for pair in range(2):
                gl = [gen_prep(2 * pair, 0), gen_prep(2 * pair + 1, 1)]
                while gl:
                    for gg in list(gl):
                        try:
                            next(gg)
                        except StopIteration:
                            gl.remove(gg)
                    step_dt()
            for _ in dgen:
                pass
            psn, pnk = psum[7], [("ps", 7)]
            lb_i = [0]

            def lbank():
                i = lb_i[0]
                lb_i[0] = (i + 1) % 3
                return psum[i], [("ps", i)]
            for tt in range(NT):
                for jh in range(4):
                    h = HB[jh]
                    pss, psk = lbank()
                    P.op("pe", lambda e, tt=tt, pss=pss, h=h: e.matmul(pss[:, 0:128], lhsT=h["kt"][:, tt * 128:(tt + 1) * 128],
                                                                       rhs=h["qt"][:, tt * 128:(tt + 1) * 128], start=True, stop=True),
                         r=h["qktk"], w=psk)
                    pt = PT8[jh * 2 + tt % 2]
                    ptk = [("PT", jh * 2 + tt % 2)]
                    P.op("dve", lambda e, pt=pt, pss=pss: e.copy_predicated(out=pt[:], mask=maskbd[:], data=pss[:, 0:128]),
                         r=psk + ["maskbd"], w=ptk)
                for cc in range(2):
                    c = 2 * tt + cc
                    for jh in range(4):
                        h = HB[jh]
                        pso, pok = psum[3 + jh], [("ps", 3 + jh)]
                        if cc == 0:
                            pt = PT8[jh * 2 + tt % 2]
                            ptk = [("PT", jh * 2 + tt % 2)]
                            P.op("pe", lambda e, tt=tt, pt=pt, jh=jh, pso=pso: e.matmul(
                                pso[:, 0:128], lhsT=vtm[:, tt * 512 + jh * 128:tt * 512 + (jh + 1) * 128],
                                rhs=pt[:], start=True, stop=False), r=ptk + vtk, w=pok)
                        P.op("pe", lambda e, c=c, cc=cc, jh=jh, h=h, pso=pso: e.matmul(
                            pso[:, cc * 64:(cc + 1) * 64], lhsT=hgSb[:, jh * 128:(jh + 1) * 128],
                            rhs=h["qh"][:, c * 64:(c + 1) * 64], start=False, stop=(cc == 1), skip_group_check=True),
                            r=h["qhkk"] + [("hgSb", jh)], w=pok)
                        pkv, pkvk = lbank()
                        P.op("pe", lambda e, cc=cc, tt=tt, pkv=pkv, jh=jh, h=h: e.matmul(
                            pkv[:, 0:128], lhsT=h["kht"][cc * 64:(cc + 1) * 64, tt, :],
                            rhs=vtm[cc * 64:(cc + 1) * 64, tt * 512 + jh * 128:tt * 512 + (jh + 1) * 128],
                            start=True, stop=True), r=h["khtk"] + vtk, w=pkvk)
                        P.op("dve", lambda e, c=c, pkv=pkv, jh=jh, h=h: e.scalar_tensor_tensor(
                            out=hgSb[:, jh * 128:(jh + 1) * 128], in0=hgS[:, l, jh * 128:(jh + 1) * 128],
                            scalar=h["ex8"][:, c:c + 1], in1=pkv[:, 0:128], op0=ALU.mult, op1=ALU.add),
                            r=pkvk + [("hgS", jh)] + h["ex8k"], w=[("hgSb", jh)])
                        P.op("dve", lambda e, c=c, pkv=pkv, jh=jh, h=h: e.scalar_tensor_tensor(
                            out=hgS[:, l, jh * 128:(jh + 1) * 128], in0=hgS[:, l, jh * 128:(jh + 1) * 128],
                            scalar=h["ex8"][:, c:c + 1], in1=pkv[:, 0:128], op0=ALU.mult, op1=ALU.add),
                            r=pkvk + [("hgS", jh)] + h["ex8k"], w=[("hgS", jh)])
                for jh in range(4):
                    pso, pok = psum[3 + jh], [("ps", 3 + jh)]
                    osb, osbk = AR(4 + jh)
                    sqb, sqbk = ARb(22 + (jh % 2))
                    sqv = sqb[:, (jh // 2) * 128:(jh // 2) * 128 + 128]
                    P.op("act", lambda e, sqv=sqv, pso=pso: e.activation(out=sqv, in_=pso[:, 0:128], func=AF.Square), r=pok, w=sqbk)
                    P.op("act", lambda e, osb=osb, pso=pso, tt=tt, jh=jh: e.activation(
                        out=osb[:, tt * 128:(tt + 1) * 128], in_=pso[:, 0:128], func=AF.Copy,
                        scale=colsF[:, cb0 + 68 + jh:cb0 + 69 + jh]), r=pok + ["colsF"], w=osbk)
                    P.op("pe", lambda e, sqv=sqv, jh=jh, tt=tt: e.matmul(psn[:, tt * 128:(tt + 1) * 128], lhsT=onesb[:], rhs=sqv,
                                                                        start=(jh == 0), stop=(jh == 3)), r=sqbk + ["onesb"], w=pnk)
            P.op("act", lambda e: e.activation(out=nrm[:], in_=psn[:, 0:T], func=AF.Ln, scale=1.0 / 512, bias=EPS),
                 r=pnk, w=["nrm"])
            P.op("act", lambda e: e.activation(out=nrm[:], in_=nrm[:], func=AF.Exp, scale=-0.5), r=["nrm"], w=["nrm"])
            for jh in range(4):
                osb, osbk = AR(4 + jh)
                sgBb, sgBk = ARb(jh // 2)
                sgB = sgBb[:, (jh % 2) * 512:(jh % 2) * 512 + T]
                P.op("dve", lambda e, osb=osb: e.tensor_tensor(out=osb, in0=osb, in1=nrm[:], op=ALU.mult),
                     r=osbk + ["nrm"], w=osbk)
                P.op("dve", lambda e, osb=osb, sgB=sgB, jh=jh: e.tensor_tensor(out=yT[:, 4 + jh, :], in0=osb, in1=sgB, op=ALU.mult),
                     r=osbk + sgBk, w=[("yT", 4 + jh)])

            xstm, xstk = ARb(6, 4)
            btm, btk = ARb(10, 1)
            for tt in range(NT):
                ps, pk = bank()
                psb = ps[:].bitcast(BF16)
                for cc in range(8):
                    P.op("pe", lambda e, cc=cc, tt=tt, psb=psb: e.transpose(
                        psb[:, cc * 128:(cc + 1) * 128], xsT[:, cc * 512 + tt * 128:cc * 512 + (tt + 1) * 128], identb[:]),
                        r=xsTk + ["identb"], w=pk)
                P.op("act", lambda e, psb=psb, tt=tt: e.activation(out=xstm[:, tt * 1024:(tt + 1) * 1024], in_=psb[:, 0:1024],
                                                                   func=AF.Copy), r=pk, w=xstk)
            ps, pk = bank()
            psb = ps[:].bitcast(BF16)
            for tt in range(NT):
                for g in range(2):
                    P.op("pe", lambda e, g=g, tt=tt, psb=psb: e.transpose(
                        psb[:, tt * 256 + g * 128:tt * 256 + (g + 1) * 128], bcT[:, g * 512 + tt * 128:g * 512 + (tt + 1) * 128],
                        identb[:]), r=bcTk + ["identb"], w=pk)
            P.op("act", lambda e, psb=psb: e.activation(out=btm[:, 0:1024], in_=psb[:, 0:1024], func=AF.Copy), r=pk, w=btk)
            def smv(base, tt, g=None):
                o = base + 16 * tt
                if g is None:
                    return sm[:, o:o + 16]
                return sm[:, o + 8 * g:o + 8 * g + 8]
            CUMC, EXPC, NEGC, EXL = 296, 360, 424, 488
            MSLOT = [21, 22, 23, 24, 32, 33, 34, 35]
            XDSLOT = [25, 26, 38, 39]
            XSSLOT = [2, 3, 4, 5]

            def p1_common(tt):
                cumc = smv(CUMC, tt)
                expc = smv(EXPC, tt)
                ps, pk = bank()
                P.op("pe", lambda e, ps=ps: e.matmul(ps[:, 0:16], lhsT=tri, rhs=dAt[:, tt * 16:(tt + 1) * 16],
                                                     start=True, stop=True), r=["cf", "sm_dA"], w=pk)
                P.op("act", lambda e, ps=ps: e.activation(out=cumc, in_=ps[:, 0:16], func=AF.Copy), r=pk, w=[("sm_cumc", tt)])
                P.op("act", lambda e: e.activation(out=expc, in_=cumc, func=AF.Exp), r=[("sm_cumc", tt)], w=[("sm_expc", tt)])
                negc = smv(NEGC, tt)
                P.op("dve", lambda e: e.tensor_scalar(out=negc, in0=cumc, scalar1=-1.0, scalar2=None, op0=ALU.mult),
                     r=[("sm_cumc", tt)], w=[("sm_negc", tt)])

            def gen_p1(tt, g):
                p2 = tt % 2
                cumc = smv(CUMC, tt)
                expc = smv(EXPC, tt)
                R, Rk = AR((17 if tt % 2 == 0 else 11) + 2 * g, 2)
                R3 = R.rearrange("p (h t) -> p h t", t=128)
                decs = R3[:, :, 127]
                negc = smv(NEGC, tt)
                exl = smv(EXL, tt, g)
                pb = [bank(), bank()]
                for hh in range(8):
                    pbh = pb[hh // 4][0][:, (hh % 4) * 128:(hh % 4 + 1) * 128]
                    col = tt * 16 + g * 8 + hh
                    P.op("pe", lambda e, pbh=pbh, col=col: e.matmul(
                        pbh, lhsT=dAt[:, col:col + 1].to_broadcast([128, 128]), rhs=tri, start=True, stop=False),
                        r=["cf", "sm_dA"], w=pb[hh // 4][1])
                    P.op("pe", lambda e, pbh=pbh, hh=hh: e.matmul(
                        pbh, lhsT=ident, rhs=negc[:, g * 8 + hh:g * 8 + hh + 1].to_broadcast([128, 128]), start=False, stop=True),
                        r=["cf", ("sm_negc", tt)], w=pb[hh // 4][1])
                for hf_ in range(2):
                    P.op("act", lambda e, hf_=hf_: e.activation(out=R[:, hf_ * 512:(hf_ + 1) * 512], in_=pb[hf_][0][:, 0:512],
                                                                func=AF.Exp), r=pb[hf_][1], w=Rk)
                yield
                P.op("dve", lambda e: e.tensor_tensor(out=exl, in0=decs, in1=expc[:, g * 8:(g + 1) * 8], op=ALU.mult),
                     r=Rk + [("sm_expc", tt)], w=[("sm_exl", tt, g)])
                psg, psgk = bank()
                P.op("pe", lambda e: e.matmul(
                    psg[:, 0:128], lhsT=bcT[:, g * 512 + tt * 128:g * 512 + (tt + 1) * 128],
                    rhs=bcT[:, (2 + g) * 512 + tt * 128:(2 + g) * 512 + (tt + 1) * 128], start=True, stop=True),
                    r=bcTk, w=psgk)
                cbm = cbuf[g][:, 0:128]
                cbmk = [("cbuf", g)]
                P.op("dve", lambda e: e.tensor_tensor(out=cbm, in0=psg[:, 0:128], in1=tri, op=ALU.mult),
                     r=psgk + ["cf"], w=cbmk)
                yield
                Mb, Mbk = ARb(MSLOT[tt * 2 + g])
                P.op("dve", lambda e: e.scalar_tensor_tensor(
                    out=Mb[:, 0:1024].rearrange("p (h t) -> p h t", t=128), in0=R3, scalar=1.0,
                    in1=cbm.unsqueeze(1).to_broadcast([128, 8, 128]), op0=ALU.min, op1=ALU.mult),
                    r=Rk + cbmk, w=Mbk)
                yield
                xdtb, xdtk = ARb(XDSLOT[tt])
                xdt = xdtb[:, g * 512:(g + 1) * 512]
                xs_g = xstm[:, tt * 1024 + g * 512:tt * 1024 + (g + 1) * 512].rearrange("p (h q) -> p h q", q=64)
                P.op("dve", lambda e: e.tensor_tensor(
                    out=xdt.rearrange("p (h q) -> p h q", q=64), in0=xs_g,
                    in1=dtt[:, tt * 16 + g * 8:tt * 16 + (g + 1) * 8].unsqueeze(2).to_broadcast([128, 8, 64]), op=ALU.mult),
                    r=xstk + ["sm_dt"], w=xdtk)
                xscb, xsck = ARb(XSSLOT[tt])
                xsc = xscb[:, g * 512:(g + 1) * 512]
                P.op("dve", lambda e: e.tensor_tensor(
                    out=xsc.rearrange("p (h q) -> p h q", q=64), in0=xdt.rearrange("p (h q) -> p h q", q=64),
                    in1=decs.unsqueeze(2).to_broadcast([128, 8, 64]), op=ALU.mult),
                    r=xdtk + Rk, w=xsck)

            def gen_p2(tt, g):
                p2 = tt % 2
                expc = smv(EXPC, tt)
                yc, yck = ARb(27)
                exl = smv(EXL, tt, g)
                xscb, xsck = ARb(XSSLOT[tt])
                xsc = xscb[:, g * 512:(g + 1) * 512]
                Mb, Mbk = ARb(MSLOT[tt * 2 + g])
                xdtb, xdtk = ARb(XDSLOT[tt])
                xdt = xdtb[:, g * 512:(g + 1) * 512]
                xs_g = xstm[:, tt * 1024 + g * 512:tt * 1024 + (g + 1) * 512].rearrange("p (h q) -> p h q", q=64)
                t1, t1k = AR(0 + g)
                t13 = t1.rearrange("p (h q) -> p h q", q=64)
                xD, xDk = AR(15 + g)
                sv = ssS[:, l, g * 512:(g + 1) * 512]
                ssq = sm[:, 256 + g:257 + g]
                psi, psik = bank()
                P.op("pe", lambda e: e.matmul(
                    psi[:, 0:512], lhsT=bcT[:, (2 + g) * 512 + tt * 128:(2 + g) * 512 + (tt + 1) * 128],
                    rhs=ssSb[:, g * 512:(g + 1) * 512], start=True, stop=True), r=bcTk + [("ssSb", g)], w=psik)
                P.op("dve", lambda e: e.tensor_tensor(
                    out=t13, in0=psi[:, 0:512].rearrange("p (h q) -> p h q", q=64),
                    in1=expc[:, g * 8:(g + 1) * 8].unsqueeze(2).to_broadcast([128, 8, 64]), op=ALU.mult),
                    r=psik + [("sm_expc", tt)], w=t1k)
                yield
                P.op("dve", lambda e: e.tensor_tensor(
                    out=sv.rearrange("p (h q) -> p h q", q=64), in0=sv.rearrange("p (h q) -> p h q", q=64),
                    in1=exl.unsqueeze(2).to_broadcast([128, 8, 64]), op=ALU.mult),
                    r=[("ssS", g), ("sm_exl", tt, g)], w=[("ssS", g)])
                pst, pstk = bank()
                P.op("pe", lambda e: e.matmul(
                    pst[:, 0:512], lhsT=btm[:, tt * 256 + g * 128:tt * 256 + (g + 1) * 128], rhs=xsc, start=True, stop=True),
                    r=btk + xsck, w=pstk)
                P.op("dve", lambda e: e.tensor_tensor(out=sv, in0=sv, in1=pst[:, 0:512], op=ALU.add),
                     r=[("ssS", g)] + pstk, w=[("ssS", g)])
                yield
                P.op("act", lambda e: e.activation(out=ssSb[:, g * 512:(g + 1) * 512], in_=sv, func=AF.Copy),
                     r=[("ssS", g)], w=[("ssSb", g)])
                yield
                psy, psyk = bank()
                for hh in range(8):
                    P.op("pe", lambda e, hh=hh: e.matmul(
                        psy[:, hh * 64:(hh + 1) * 64], lhsT=Mb[:, hh * 128:(hh + 1) * 128], rhs=xdt[:, hh * 64:(hh + 1) * 64],
                        start=True, stop=True), r=Mbk + xdtk, w=psyk)
                P.op("dve", lambda e: e.tensor_tensor(out=t1, in0=t1, in1=psy[:, 0:512], op=ALU.add),
                     r=t1k + psyk, w=t1k)
                yield
                P.op("dve", lambda e: e.tensor_tensor(
                    out=xD.rearrange("p (h q) -> p h q", q=64), in0=xs_g,
                    in1=rowb[:, 1056 + g * 8:1056 + (g + 1) * 8].unsqueeze(2).to_broadcast([128, 8, 64]), op=ALU.mult),
                    r=xstk + ["rowb"], w=xDk)
                P.op("dve", lambda e: e.tensor_tensor(out=t1, in0=t1, in1=xD, op=ALU.add), r=t1k + xDk, w=t1k)
                P.op("dve", lambda e: e.tensor_tensor(
                    out=t1, in0=t1, in1=sztm[:, tt * 1024 + g * 512:tt * 1024 + (g + 1) * 512], op=ALU.mult),
                    r=t1k + szk, w=t1k)
                yield
                P.op("act", lambda e: e.activation(out=xD, in_=t1, func=AF.Square, accum_out=ssq),
                     r=t1k, w=xDk + [("sm_ssq", g)])
                P.op("act", lambda e: e.activation(out=ssq, in_=ssq, func=AF.Ln, scale=1.0 / 512, bias=EPS),
                     r=[("sm_ssq", g)], w=[("sm_ssq", g)])
                P.op("act", lambda e: e.activation(out=ssq, in_=ssq, func=AF.Exp, scale=-0.5),
                     r=[("sm_ssq", g)], w=[("sm_ssq", g)])
                P.op("act", lambda e: e.activation(out=t1, in_=t1, func=AF.Copy, scale=ssq),
                     r=t1k + [("sm_ssq", g)], w=t1k)
                yield
                P.op("dve", lambda e: e.tensor_tensor(out=yc[:, g * 512:(g + 1) * 512], in0=t1,
                                                      in1=rowb[:, g * 512:(g + 1) * 512], op=ALU.mult),
                     r=t1k + ["rowb"], w=yck)

            def p2_final(tt):
                yc, yck = ARb(27)
                ps, pk = bank()
                psb = ps[:].bitcast(BF16)
                for kc in range(8):
                    P.op("pe", lambda e, kc=kc: e.transpose(psb[:, kc * 128:(kc + 1) * 128],
                                                            yc[:, kc * 128:(kc + 1) * 128], identb[:]),
                         r=yck + ["identb"], w=pk)
                P.op("act", lambda e: e.activation(out=yT[:, 8:16, tt * 128:(tt + 1) * 128],
                                                   in_=psb[:, 0:1024].rearrange("p (a b) -> p a b", b=128),
                                                   func=AF.Copy), r=pk, w=[("yT", c) for c in range(8, 16)])

            def run_rr(gens):
                gens = list(gens)
                while gens:
                    for gg in list(gens):
                        try:
                            next(gg)
                        except StopIteration:
                            gens.remove(gg)

            p1_common(0)
            run_rr([gen_p1(0, 0), gen_p1(0, 1)])
            ahead = {}
            for tt in range(1, NT):
                p1_common(tt)
                ahead[tt] = [gen_p1(tt, 0), gen_p1(tt, 1)]

            def step_all(gl):
                for gg in list(gl):
                    try:
                        next(gg)
                    except StopIteration:
                        gl.remove(gg)
            for tt in range(NT):
                cur = [gen_p2(tt, 0), gen_p2(tt, 1)]
                while cur:
                    step_all(cur)
                    for t2 in range(tt + 1, NT):
                        if ahead[t2]:
                            step_all(ahead[t2])
                            break
                if tt + 1 < NT:
                    while ahead[tt + 1]:
                        step_all(ahead[tt + 1])
                p2_final(tt)

            if dbg and j == nseg - 1 and l == nlayers - 1:
                P.dma("sp", dbg_y[:, :], yT[:].rearrange("p a b -> p (a b)"), r=ykeys, w=["dbg_y"])

            for dm in range(8):
                wo, wok = load_wout(l, dm)
                ps, pk = bank()
                for kc in range(16):
                    P.op("pe", lambda e, kc=kc, ps=ps, wo=wo: e.matmul(ps[:, 0:T], lhsT=wo[:, kc, :], rhs=yT[:, kc, :],
                                                                       start=(kc == 0), stop=(kc == 15)),
                         r=wok + [("yT", kc)], w=pk)
                P.op("dve", lambda e, dm=dm, ps=ps: e.scalar_tensor_tensor(out=xT[:, dm, :], in0=ps[:, 0:T],
                                                                           scalar=dcol[:, l, 16 + dm:17 + dm], in1=xT[:, dm, :],
                                                                           op0=ALU.mult, op1=ALU.add),
                     r=pk + [("xT", dm), "dcol_all"], w=[("xT", dm)])

        def final_out(j):
            norm_phase(colsF[:, CB:CB + 8], None, False)
            for tt in range(NT):
                ob, obk = AR(8 + 2 * (tt % 2), 2)
                for hf_ in range(2):
                    ps, pk = bank()
                    for q4 in range(4):
                        c = hf_ * 4 + q4
                        src, sk = AR(c)
                        P.op("pe", lambda e, q4=q4, tt=tt, ps=ps, src=src: e.transpose(ps[:, q4 * 128:(q4 + 1) * 128],
                                                                                      src[:, tt * 128:(tt + 1) * 128], ident),
                             r=sk + ["cf"], w=pk)
                    P.op("act" if hf_ else "dve",
                         (lambda e, ps=ps, ob=ob, hf_=hf_: e.activation(out=ob[:, hf_ * 512:(hf_ + 1) * 512], in_=ps[:, 0:512], func=AF.Copy))
                         if hf_ else
                         (lambda e, ps=ps, ob=ob, hf_=hf_: e.tensor_copy(out=ob[:, hf_ * 512:(hf_ + 1) * 512], in_=ps[:, 0:512])),
                         r=pk, w=obk)
                P.dma("sp", out_d[j * T + tt * 128:j * T + (tt + 1) * 128, :], ob, r=obk, w=[("out", j, tt)])

        P._commit(("dve", P.cnt["dve"]), [], ["dcol_all"])

        for j in range(nseg):
            if j == 0:
                load_x_dma(0)
            load_x_tr(j)
            for l in range(nlayers):
                unit(j, l)
            if dbg and j == nseg - 1:
                P.dma("sp", dbg_x[:, :], xT[:].rearrange("p a b -> p (a b)"), r=xkeys, w=["dbg_x"])
            if j + 1 < nseg:
                load_x_dma(j + 1)
            final_out(j)

        okeys = [("out", j, tt) for j in range(nseg) for tt in range(NT)]
        if dbg:
            okeys += ["dbg_x", "dbg_y"]
        P.wait_all("sp", okeys)
        for i in range(P.NDS):
            if P.dval[i] > 0:
                P._wait("sp", {(("d", i), P.dval[i])})
        print("instr counts", P.cnt)
    return nc


def _host_layout(inputs):
    f32 = np.float32
    g = {k: np.asarray(v, f32) for k, v in inputs.items()}

    def fm(v, n):
        return np.ascontiguousarray(v.reshape(n, 128).T)

    cols_shared = np.zeros((128, NCOLS), f32)
    for l in range(DEPTH):
        b = l * LC
        cols_shared[:, b + 0:b + 8] = fm(g["norm_w"][l], 8)
        cols_shared[:, b + 8:b + 32] = fm(g["b_ada"][l], 24)
        for jc in range(4):
            for k in range(4):
                cols_shared[:, b + 32 + jc * 4 + k] = g["lru_conv_w"][l, k, jc * 128:(jc + 1) * 128]
        cols_shared[:, b + 48:b + 52] = fm(g["lru_conv_b"][l], 4)
        cols_shared[:, b + 52:b + 56] = fm(g["lru_ba"][l].reshape(512), 4)
        cols_shared[:, b + 56:b + 60] = fm(g["lru_bx"][l].reshape(512), 4)
        cols_shared[:, b + 60:b + 64] = fm(g["lru_lambda"][l], 4)
        cols_shared[:, b + 64:b + 68] = fm(g["hg_lb_logits"][l], 4)
        cols_shared[:, b + 68:b + 72] = fm(g["hg_norm_w"][l], 4)
        for cc in range(12):
            for k in range(4):
                cols_shared[:, b + 72 + cc * 4 + k] = g["ssd_conv_w"][l, k, cc * 128:(cc + 1) * 128]
        cols_shared[:, b + 120:b + 132] = fm(g["ssd_conv_b"][l], 12)
    CBb = LC * DEPTH
    cols_shared[:, CBb:CBb + 8] = fm(g["final_norm_w"], 8)
    rowp = np.zeros((DEPTH, 1072), f32)
    rowp[:, 0:1024] = g["ssd_norm_w"]
    rowp[:, 1024:1040] = g["ssd_dt_bias"]
    rowp[:, 1040:1056] = g["ssd_a_log"]
    rowp[:, 1056:1072] = g["ssd_d"]
    lrubd = np.zeros((DEPTH, 128, 8, 128), f32)
    for l in range(DEPTH):
        for m, nm in enumerate(("lru_wa", "lru_wx")):
            for jc in range(4):
                for hh in range(2):
                    lrubd[l, hh * 64:(hh + 1) * 64, m * 4 + jc, hh * 64:(hh + 1) * 64] = g[nm][l, jc * 2 + hh]
    lrubd = lrubd.reshape(DEPTH, 128, 1024)
    constf = np.zeros((128, 384), f32)
    constf[:, 0:128] = np.eye(128, dtype=f32)
    constf[:, 128:256] = np.triu(np.ones((128, 128), f32))
    constf[:, 256:384] = 1.0
    idx = np.arange(128)
    constm = ((idx[:, None] <= idx[None, :]) & ((idx[:, None] // 64) == (idx[None, :] // 64))).astype(np.uint32)
    in_maps = []
    for b in range(g["x"].shape[0]):
        cols = cols_shared.copy()
        cols[:, CBb + 8:CBb + 16] = fm(g["c"][b], 8)
        in_maps.append({"x": np.ascontiguousarray(g["x"][b]), "w_ada": g["w_ada"], "w_in": g["w_in"], "w_out": g["w_out"],
                        "colsF": cols, "rowp": rowp, "lrubd": lrubd, "constf": constf, "constm": constm})
    return in_maps


_NC_CACHE = {}


def kernel(**inputs):
    in_maps = _host_layout(inputs)
    if "nc" not in _NC_CACHE:
        _NC_CACHE["nc"] = _build()
    nc = _NC_CACHE["nc"]
    res = run_bass_kernel_spmd(nc, in_maps, core_ids=list(range(8)))
    out = np.stack([np.asarray(r["out"], np.float32) for r in res.results], axis=0)
    return out
```

```python
import numpy as np
from contextlib import ExitStack
import concourse.bass as bass
import concourse.mybir as mybir
from concourse.bass_utils import run_bass_kernel_spmd

F32 = mybir.dt.float32
BF16 = mybir.dt.bfloat16
U32 = mybir.dt.uint32
AF = mybir.ActivationFunctionType
ALU = mybir.AluOpType

D = 1024
S = 2048
DEPTH = 4
NIN = 5648
T = 512
NT = T // 128
EPS = 1e-6
LC = 132
NCOLS = LC * DEPTH + 16
ND = 52
NSLOT = 40
OFF = dict(ax=0, ag=512, q=1024, f=1536, i=2048, g=2560, z=3072, xs=4096, B=5120, C=5376, dt=5632)


class Prog:
    def __init__(self, nc, st):
        self.nc = nc
        self.st = st
        self.eng = {"pe": nc.tensor, "act": nc.scalar, "dve": nc.vector, "pool": nc.gpsimd, "sp": nc.sync}
        self.sem = {e: st.enter_context(nc.semaphore("sem_" + e)) for e in ("pe", "act", "dve", "pool")}
        self.cnt = {e: 0 for e in self.sem}
        self.NDS = 12
        self.dsem = [st.enter_context(nc.semaphore("dsem%d" % i)) for i in range(self.NDS)]
        self.dval = [0] * self.NDS
        self.dnext = 0
        self.water = {e: {} for e in self.eng}
        self.lastw = {}
        self.readers = {}
        self.bank_i = 0

    def _semh(self, s):
        return self.sem[s] if isinstance(s, str) else self.dsem[s[1]]

    def _wait(self, e, deps):
        need = {}
        for (s, v) in deps:
            if s == e and e == "pe":
                continue
            if v > need.get(s, 0):
                need[s] = v
        for s, v in need.items():
            if self.water[e].get(s, 0) >= v:
                continue
            self.eng[e].wait_ge(self._semh(s), v)
            self.water[e][s] = v

    def _deps(self, r, w):
        deps = set()
        for k in r:
            t = self.lastw.get(k)
            if t:
                deps.add(t)
        for k in w:
            t = self.lastw.get(k)
            if t:
                deps.add(t)
            for t in self.readers.get(k, ()):
                deps.add(t)
        return deps

    def _commit(self, tok, r, w):
        for k in r:
            self.readers.setdefault(k, []).append(tok)
        for k in w:
            self.lastw[k] = tok
            self.readers[k] = []

    def op(self, e, fn, r=(), w=()):
        r = list(r)
        w = list(w)
        self._wait(e, self._deps(r, w))
        ins = fn(self.eng[e])
        self.cnt[e] += 1
        ins.then_inc(self.sem[e], 1)
        self._commit((e, self.cnt[e]), r, w)

    def dma(self, q, out, in_, r=(), w=()):
        r = list(r)
        w = list(w)
        i = self.dnext
        self.dnext = (self.dnext + 1) % self.NDS
        deps = self._deps(r, w)
        if self.dval[i] > 0:
            deps.add((("d", i), self.dval[i]))
        self._wait(q, deps)
        self.dval[i] += 16
        self.eng[q].dma_start(out=out, in_=in_).then_inc(self.dsem[i], 16)
        self._commit((("d", i), self.dval[i]), r, w)

    def wait_all(self, e, keys):
        deps = set()
        for k in keys:
            t = self.lastw.get(k)
            if t:
                deps.add(t)
        self._wait(e, deps)


def _build(nseg=S // T, nlayers=DEPTH, dbg=False):
    nc = bass.Bass("TRN2", target_bir_lowering=False)
    x_d = nc.dram_tensor("x", [S, D], F32, kind="ExternalInput").ap()
    wada_d = nc.dram_tensor("w_ada", [DEPTH, D, 3 * D], F32, kind="ExternalInput").ap()
    win_d = nc.dram_tensor("w_in", [DEPTH, D, NIN], F32, kind="ExternalInput").ap()
    wout_d = nc.dram_tensor("w_out", [DEPTH, 2 * D, D], F32, kind="ExternalInput").ap()
    cols_d = nc.dram_tensor("colsF", [128, NCOLS], F32, kind="ExternalInput").ap()
    rowp_d = nc.dram_tensor("rowp", [DEPTH, 1072], F32, kind="ExternalInput").ap()
    lrubd_d = nc.dram_tensor("lrubd", [DEPTH, 128, 1024], F32, kind="ExternalInput").ap()
    cf_d = nc.dram_tensor("constf", [128, 384], F32, kind="ExternalInput").ap()
    cm_d = nc.dram_tensor("constm", [128, 128], U32, kind="ExternalInput").ap()
    out_d = nc.dram_tensor("out", [S, D], F32, kind="ExternalOutput").ap()
    if dbg:
        dbg_y = nc.dram_tensor("dbg_y", [128, 16 * T], BF16, kind="ExternalOutput").ap()
        dbg_x = nc.dram_tensor("dbg_x", [128, 8 * T], F32, kind="ExternalOutput").ap()

    with ExitStack() as st:
        P = Prog(nc, st)

        def sb(name, shape, dt):
            return st.enter_context(nc.sbuf_tensor(name, shape, dt))

        xT = sb("xT", [128, 8, T], F32)
        hT = sb("hT", [128, 8, T], BF16)
        yT = sb("yT", [128, 16, T], BF16)
        NW = 8
        wbf = [sb("wbf%d" % i, [128, 2048], BF16) for i in range(NW)]
        colsF = sb("colsF_sb", [128, NCOLS], F32)
        dcol = sb("dcol", [128, DEPTH, ND], F32)
        lrub = sb("lrub", [128, 1024], BF16)
        cf = sb("cf", [128, 384], F32)
        identb = sb("identb", [128, 128], BF16)
        onesb = sb("onesb", [128, 128], BF16)
        maskbd = sb("maskbd", [128, 128], U32)
        rmask = sb("rmask", [128, T], F32)
        rowb = sb("rowb", [128, 1072], F32)
        abc = sb("abc", [128, 16], F32)
        nrm = sb("nrm", [128, T], F32)
        cbuf = [sb("cbuf%d" % i, [128, T + 3], F32) for i in range(2)]
        PT8 = [sb("PT%d" % i, [128, 128], BF16) for i in range(8)]
        khtm4 = [sb("khtm%d" % i, [128, 4, 128], BF16) for i in range(4)]
        smalls = sb("smalls", [128, 560], F32)
        arena = sb("arena", [128, NSLOT * 512], F32)
        arena_b = arena[:].bitcast(BF16)
        lru_tail = sb("lru_tail", [128, DEPTH, 12], F32)
        lru_h = sb("lru_h", [128, DEPTH, 4], F32)
        ssd_tail = sb("ssd_tail", [128, DEPTH, 36], F32)
        hgS = sb("hgS", [128, DEPTH, 512], F32)
        ssS = sb("ssS", [128, DEPTH, 1024], F32)
        hgSb = sb("hgSb", [128, 512], BF16)
        ssSb = sb("ssSb", [128, 1024], BF16)
        psum = [st.enter_context(nc.psum_tensor("ps%d" % i, [128, 512], F32)) for i in range(8)]

        ident = cf[:, 0:128]
        tri = cf[:, 128:256]
        onesf = cf[:, 256:384]

        def bank():
            i = P.bank_i
            P.bank_i = (P.bank_i + 1) % 8
            return psum[i], [("ps", i)]

        def AR(i, n=1):
            return arena[:, i * 512:(i + n) * 512], [("ar", k) for k in range(i, i + n)]

        def ARb(i, n=1):
            return arena_b[:, i * 1024:(i + n) * 1024], [("ar", k) for k in range(i, i + n)]

        wctr = [0, 0]

        WPOOL = {"m": (0, 5), "c": (5, 3)}
        wpc = {"m": 0, "c": 0}

        def load_w(src, nfree, shape3, pool="m"):
            base, n = WPOOL[pool]
            wi = base + wpc[pool] % n
            wpc[pool] += 1
            a, b = shape3
            wv = wbf[wi][:, 0:nfree].rearrange("p (a b) -> p a b", a=a)
            P.dma("pool", wv, src, r=[], w=[("wbf", wi)])
            return wv, [("wbf", wi)]

        def load_win(l, c0, ncol=256, pool="m"):
            src = win_d[l].rearrange("(k p) n -> p k n", p=128)[:, :, c0:c0 + ncol]
            return load_w(src, 8 * ncol, (8, ncol), pool)

        def load_wout(l, dm):
            src = wout_d[l].rearrange("(k p) n -> p k n", p=128)[:, :, dm * 128:(dm + 1) * 128]
            return load_w(src, 16 * 128, (16, 128))

        hkeys = [("hT", c) for c in range(8)]

        def fm_mm(ps, pk, wv, wk, col0):
            for k in range(8):
                P.op("pe", lambda e, k=k: e.matmul(ps[:, 0:T], lhsT=wv[:, k, col0:col0 + 128], rhs=hT[:, k, :],
                                                   start=(k == 0), stop=(k == 7)),
                     r=wk + [("hT", k)], w=pk)

        def tm_mm(ps, pk, wv, wk, tt, c0, ncol, o0):
            for k in range(8):
                P.op("pe", lambda e, k=k: e.matmul(ps[:, o0:o0 + ncol], lhsT=hT[:, k, tt * 128:(tt + 1) * 128],
                                                   rhs=wv[:, k, c0:c0 + ncol], start=(k == 0), stop=(k == 7)),
                     r=wk + [("hT", k)], w=pk)

        P.dma("sp", colsF[:], cols_d[:, :], w=["colsF"])
        P.dma("sp", cf[:], cf_d[:, :], w=["cf"])
        P.dma("sp", maskbd[:], cm_d[:, :], w=["maskbd"])
        P.op("dve", lambda e: e.tensor_copy(out=identb[:], in_=ident), r=["cf"], w=["identb"])
        P.op("dve", lambda e: e.tensor_copy(out=onesb[:], in_=onesf), r=["cf"], w=["onesb"])
        P.op("dve", lambda e: e.memset(rmask[:], 1.0), w=["rmask"])
        P.op("dve", lambda e: e.memset(rmask[:].rearrange("p (c j) -> p c j", j=64)[:, :, 0:1], 0.0), w=["rmask"])
        for i in range(8):
            P.op("dve", lambda e, i=i: e.memset(PT8[i][:], 0.0), w=[("PT", i)])
        P.op("pool", lambda e: e.memset(lru_tail[:], 0.0), w=["lru_tail"])
        P.op("pool", lambda e: e.memset(lru_h[:], 0.0), w=["lru_h"])
        P.op("pool", lambda e: e.memset(ssd_tail[:], 0.0), w=["ssd_tail"])
        P.op("pool", lambda e: e.memset(hgS[:], 0.0), w=[("hgS", i) for i in range(4)])
        P.op("pool", lambda e: e.memset(ssS[:], 0.0), w=[("ssS", i) for i in range(2)])

        CB = LC * DEPTH
        sm = smalls
        P.op("act", lambda e: e.activation(out=sm[:, 0:8], in_=colsF[:, CB + 8:CB + 16], func=AF.Silu),
             r=["colsF"], w=["sm_cond"])
        for l in range(DEPTH):
            P.op("act", lambda e, l=l: e.activation(out=sm[:, 16 + 4 * l:20 + 4 * l],
                                                    in_=colsF[:, l * LC + 64:l * LC + 68], func=AF.Exp),
                 r=["colsF"], w=["sm_ex"])
        P.op("dve", lambda e: e.tensor_tensor(out=sm[:, 32:36], in0=sm[:, 16:20], in1=sm[:, 20:24], op=ALU.add),
             r=["sm_ex"], w=["sm_tot"])
        P.op("dve", lambda e: e.tensor_tensor(out=sm[:, 32:36], in0=sm[:, 32:36], in1=sm[:, 24:28], op=ALU.add),
             r=["sm_ex", "sm_tot"], w=["sm_tot"])
        P.op("dve", lambda e: e.tensor_tensor(out=sm[:, 32:36], in0=sm[:, 32:36], in1=sm[:, 28:32], op=ALU.add),
             r=["sm_ex", "sm_tot"], w=["sm_tot"])
        P.op("dve", lambda e: e.reciprocal(out=sm[:, 36:40], in_=sm[:, 32:36]), r=["sm_tot"], w=["sm_rt"])
        P.op("dve", lambda e: e.memset(dcol[:, 0, 40:44], 0.0), w=[("dcol", 0)])
        for l in range(1, DEPTH):
            P.op("dve", lambda e, l=l: e.tensor_tensor(out=sm[:, 40:44], in0=sm[:, 16 + 4 * l:20 + 4 * l],
                                                       in1=sm[:, 36:40], op=ALU.mult),
                 r=["sm_ex", "sm_rt"], w=["sm_p"])
            P.op("dve", lambda e, l=l: e.tensor_tensor(out=dcol[:, l, 40:44], in0=dcol[:, l - 1, 40:44],
                                                       in1=sm[:, 40:44], op=ALU.add),
                 r=["sm_p", ("dcol", l - 1)], w=[("dcol", l)])
        for l in range(DEPTH):
            dk = [("dcol", l)]
            P.op("dve", lambda e, l=l: e.tensor_scalar(out=dcol[:, l, 44:48], in0=dcol[:, l, 40:44], scalar1=-0.5,
                                                       scalar2=0.5, op0=ALU.mult, op1=ALU.add), r=dk, w=dk)
            P.op("dve", lambda e, l=l: e.tensor_scalar(out=dcol[:, l, 48:52], in0=dcol[:, l, 40:44], scalar1=0.5,
                                                       scalar2=0.5, op0=ALU.mult, op1=ALU.add), r=dk, w=dk)
            P.op("dve", lambda e, l=l: e.tensor_scalar(out=dcol[:, l, 24:32], in0=colsF[:, l * LC + 52:l * LC + 60],
                                                       scalar1=0.5, scalar2=None, op0=ALU.mult), r=dk + ["colsF"], w=dk)
            P.op("act", lambda e, l=l: e.activation(out=sm[:, 48:52], in_=colsF[:, l * LC + 60:l * LC + 64],
                                                    func=AF.Exp, scale=-1.0), r=["colsF"], w=["sm_sp"])
            P.op("act", lambda e: e.activation(out=sm[:, 48:52], in_=sm[:, 48:52], func=AF.Ln, bias=1.0),
                 r=["sm_sp"], w=["sm_sp"])
            P.op("dve", lambda e, l=l: e.tensor_scalar(out=dcol[:, l, 32:36], in0=sm[:, 48:52], scalar1=-4.0,
                                                       scalar2=None, op0=ALU.mult), r=dk + ["sm_sp"], w=dk)
            P.op("dve", lambda e, l=l: e.tensor_scalar(out=dcol[:, l, 36:40], in0=sm[:, 48:52], scalar1=-8.0,
                                                       scalar2=None, op0=ALU.mult), r=dk + ["sm_sp"], w=dk)

        modrow, mrk = AR(0, 6)
        one11 = onesf[0:1, 0:1]
        for l in range(nlayers):
            for s in range(12):
                si = wctr[0] % 2
                wctr[0] += 1
                src = wada_d[l].rearrange("(k p) n -> p k n", p=128)[:, :, s * 256:(s + 1) * 256]
                stv, stk = AR(8 + 4 * si, 4)
                sv = stv.rearrange("p (a b) -> p a b", a=8)
                P.dma("sp", sv, src, w=stk)
                ps, pk = bank()
                for k in range(8):
                    P.op("pe", lambda e, k=k, sv=sv, ps=ps: e.matmul(ps[0:1, 0:256], lhsT=sm[:, k:k + 1], rhs=sv[:, k, :],
                                                                      start=(k == 0), stop=(k == 7)),
                         r=stk + ["sm_cond"], w=pk)
                P.op("act", lambda e, s=s, ps=ps: e.activation(out=modrow[0:1, s * 256:(s + 1) * 256], in_=ps[0:1, 0:256],
                                                               func=AF.Copy), r=pk, w=mrk)
            ps, pk = bank()
            for c in range(24):
                P.op("pe", lambda e, c=c, ps=ps: e.matmul(ps[:, c:c + 1], lhsT=modrow[0:1, c * 128:(c + 1) * 128], rhs=one11,
                                                          start=True, stop=True), r=mrk + ["cf"], w=pk)
            dk = [("dcol", l)]
            P.op("dve", lambda e, l=l, ps=ps: e.tensor_tensor(out=dcol[:, l, 8:16], in0=ps[:, 0:8],
                                                              in1=colsF[:, l * LC + 8:l * LC + 16], op=ALU.add),
                 r=pk + ["colsF"], w=dk)
            P.op("dve", lambda e, l=l, ps=ps: e.tensor_tensor(out=dcol[:, l, 16:24], in0=ps[:, 16:24],
                                                              in1=colsF[:, l * LC + 24:l * LC + 32], op=ALU.add),
                 r=pk + ["colsF"], w=dk)
            P.op("dve", lambda e, l=l, ps=ps: e.tensor_tensor(out=sm[:, 56:64], in0=ps[:, 8:16],
                                                              in1=colsF[:, l * LC + 16:l * LC + 24], op=ALU.add),
                 r=pk + ["colsF"], w=["sm_sc"])
            P.op("dve", lambda e, l=l: e.scalar_tensor_tensor(out=dcol[:, l, 0:8], in0=sm[:, 56:64], scalar=1.0,
                                                              in1=colsF[:, l * LC:l * LC + 8], op0=ALU.add, op1=ALU.mult),
                 r=["sm_sc", "colsF"], w=dk)

        xkeys = [("xT", c) for c in range(8)]
        ykeys = [("yT", c) for c in range(16)]

        def rms_bcast(src_keys, nfeat):
            pass

        def load_x_dma(j):
            for tt in range(NT):
                xin, xk = AR(16 + tt * 2, 2)
                P.dma("sp", xin, x_d[j * T + tt * 128:j * T + (tt + 1) * 128, :], w=xk)

        def load_x_tr(j):
            for c in range(8):
                ps, pk = bank()
                for tt in range(NT):
                    xin, xk = AR(16 + tt * 2, 2)
                    P.op("pe", lambda e, tt=tt, xin=xin, ps=ps: e.transpose(ps[:, tt * 128:(tt + 1) * 128],
                                                                            xin[:, c * 128:(c + 1) * 128], ident),
                         r=xk + ["cf"], w=pk)
                P.op("act" if c % 2 else "dve",
                     (lambda e, ps=ps, c=c: e.activation(out=xT[:, c, :], in_=ps[:, 0:T], func=AF.Copy)) if c % 2 else
                     (lambda e, ps=ps, c=c: e.tensor_copy(out=xT[:, c, :], in_=ps[:, 0:T])),
                     r=pk, w=[("xT", c)])

        def norm_phase(gcol, shcol, dst_bf):
            for hf_ in range(2):
                P.op("act", lambda e, hf_=hf_: e.activation(out=hT[:, hf_ * 4:(hf_ + 1) * 4, :].rearrange("p a b -> p (a b)"),
                                                            in_=xT[:, hf_ * 4:(hf_ + 1) * 4, :].rearrange("p a b -> p (a b)"),
                                                            func=AF.Square),
                     r=xkeys[hf_ * 4:(hf_ + 1) * 4], w=hkeys[hf_ * 4:(hf_ + 1) * 4])
            ps, pk = bank()
            for c in range(8):
                P.op("pe", lambda e, c=c: e.matmul(ps[:, 0:T], lhsT=onesb[:], rhs=hT[:, c, :], start=(c == 0), stop=(c == 7)),
                     r=[("hT", c), "onesb"], w=pk)
            P.op("act", lambda e: e.activation(out=nrm[:], in_=ps[:, 0:T], func=AF.Ln, scale=1.0 / D, bias=EPS),
                 r=pk, w=["nrm"])
            P.op("act", lambda e: e.activation(out=nrm[:], in_=nrm[:], func=AF.Exp, scale=-0.5), r=["nrm"], w=["nrm"])
            for c in range(8):
                if dst_bf:
                    tmp, tk = AR(26 + (c % 2))
                    P.op("dve", lambda e, c=c, tmp=tmp: e.tensor_tensor(out=tmp, in0=xT[:, c, :], in1=nrm[:], op=ALU.mult),
                         r=[("xT", c), "nrm"], w=tk)
                    P.op("act", lambda e, c=c, tmp=tmp: e.activation(out=hT[:, c, :], in_=tmp, func=AF.Identity,
                                                                     scale=gcol[:, c:c + 1], bias=shcol[:, c:c + 1]),
                         r=tk + ["dcol_all"], w=[("hT", c)])
                else:
                    tmp, tk = AR(c)
                    P.op("dve", lambda e, c=c, tmp=tmp: e.tensor_tensor(out=tmp, in0=xT[:, c, :], in1=nrm[:], op=ALU.mult),
                         r=[("xT", c), "nrm"], w=tk)
                    P.op("act", lambda e, c=c, tmp=tmp: e.activation(out=tmp, in_=tmp, func=AF.Copy, scale=gcol[:, c:c + 1]),
                         r=tk + ["colsF"], w=tk)

        def conv_chunk(ps, pk, tail_ap, tail_key, wcol, dst, dk, ci):
            cb = cbuf[ci % 2]
            ck = [("cbuf", ci % 2)]
            P.op("act", lambda e: e.activation(out=cb[:, 3:3 + T], in_=ps[:, 0:T], func=AF.Copy), r=pk, w=ck)
            P.op("act", lambda e: e.activation(out=dst, in_=ps[:, 0:T], func=AF.Copy, scale=wcol[:, 3:4]),
                 r=pk + ["colsF"], w=dk)
            P.op("dve", lambda e: e.tensor_copy(out=cb[:, 0:3], in_=tail_ap), r=[tail_key], w=ck)
            P.op("dve", lambda e: e.tensor_copy(out=tail_ap, in_=cb[:, T:T + 3]), r=ck, w=[tail_key])
            for k in range(3):
                P.op("dve", lambda e, k=k: e.scalar_tensor_tensor(out=dst, in0=cb[:, k:k + T], scalar=wcol[:, k:k + 1],
                                                                  in1=dst, op0=ALU.mult, op1=ALU.add),
                     r=ck + dk + ["colsF"], w=dk)

        def unit(j, l):
            cb0 = l * LC
            dc = dcol[:, l, :]
            P.dma("sp", rowb[:], rowp_d[l:l + 1, :].partition_broadcast(128), w=["rowb"])
            P.dma("pool", lrub[:], lrubd_d[l], w=["lrub"])
            P.op("act", lambda e: e.activation(out=hgSb[:], in_=hgS[:, l, :], func=AF.Copy),
                 r=[("hgS", i) for i in range(4)], w=[("hgSb", i) for i in range(4)])
            P.op("act", lambda e: e.activation(out=ssSb[:], in_=ssS[:, l, :], func=AF.Copy),
                 r=[("ssS", i) for i in range(2)], w=[("ssSb", i) for i in range(2)])

            norm_phase(dcol[:, l, 0:8], dcol[:, l, 8:16], True)

            P.op("act", lambda e: e.activation(out=abc[:], in_=rowb[:, 1040:1056], func=AF.Exp), r=["rowb"], w=["abc"])
            P.op("dve", lambda e: e.tensor_scalar(out=abc[:], in0=abc[:], scalar1=-1.0, scalar2=None, op0=ALU.mult),
                 r=["abc"], w=["abc"])

            sztm, szk = ARb(28, 4)
            xsT, xsTk = ARb(32, 4)
            bcT, bcTk = ARb(36, 2)
            dtt = sm[:, 80:144]
            dAt = sm[:, 144:208]

            def gen_cpre():
                for hf_ in range(2):
                    wz = [load_win(l, OFF["z"] + 256 * (hf_ * 2 + i), pool="c") for i in range(2)]
                    for tt in range(NT):
                        ps, pk = bank()
                        for q4 in range(2):
                            tm_mm(ps, pk, wz[q4][0], wz[q4][1], tt, 0, 256, q4 * 256)
                        P.op("act", lambda e, ps=ps, tt=tt, hf_=hf_: e.activation(
                            out=sztm[:, tt * 1024 + hf_ * 512:tt * 1024 + (hf_ + 1) * 512], in_=ps[:, 0:512], func=AF.Silu),
                            r=pk, w=szk)
                        yield
                pend_c = [None]
                for cc in range(12):
                    if cc % 2 == 0:
                        wv, wk = load_win(l, OFF["xs"] + 128 * cc, pool="c")
                    ps, pk = bank()
                    fm_mm(ps, pk, wv, wk, (cc % 2) * 128)
                    tail = ssd_tail[:, l, cc * 3:cc * 3 + 3]
                    xc, xck = AR(38 + (cc % 2))
                    wcol = colsF[:, cb0 + 72 + cc * 4:cb0 + 76 + cc * 4]
                    conv_chunk(ps, pk, tail, "ssd_tail", wcol, xc, xck, cc)
                    if cc < 8:
                        dst = xsT[:, cc * 512:(cc + 1) * 512]
                        dk_ = xsTk
                    else:
                        dst = bcT[:, (cc - 8) * 512:(cc - 7) * 512]
                        dk_ = bcTk

                    def fin_c(dst=dst, xc=xc, cc=cc, xck=xck, dk_=dk_):
                        P.op("act", lambda e: e.activation(out=dst, in_=xc, func=AF.Silu,
                                                           bias=colsF[:, cb0 + 120 + cc:cb0 + 121 + cc]),
                             r=xck + ["colsF"], w=dk_)
                    if pend_c[0] is not None:
                        pend_c[0]()
                    pend_c[0] = fin_c
                    yield
                pend_c[0]()

            def gen_dt():
                wdt = load_win(l, OFF["dt"], 16, pool="c")
                for tt in range(NT):
                    ps, pk = bank()
                    tm_mm(ps, pk, wdt[0], wdt[1], tt, 0, 16, 0)
                    P.op("dve", lambda e, ps=ps, tt=tt: e.tensor_tensor(out=dtt[:, tt * 16:(tt + 1) * 16], in0=ps[:, 0:16],
                                                                        in1=rowb[:, 1024:1040], op=ALU.add),
                         r=pk + ["rowb"], w=["sm_dt"])
                    yield
                P.op("act", lambda e: e.activation(out=dtt, in_=dtt, func=AF.Exp), r=["sm_dt"], w=["sm_dt"])
                P.op("act", lambda e: e.activation(out=dtt, in_=dtt, func=AF.Ln, bias=1.0), r=["sm_dt"], w=["sm_dt"])
                P.op("dve", lambda e: e.tensor_tensor(out=dAt.rearrange("p (a b) -> p a b", b=16),
                                                      in0=dtt.rearrange("p (a b) -> p a b", b=16),
                                                      in1=abc[:].unsqueeze(1).to_broadcast([128, 4, 16]), op=ALU.mult),
                     r=["sm_dt", "abc"], w=["sm_dA"])

            cgen = gen_cpre()
            dgen = gen_dt()

            def step_c():
                next(cgen, None)

            def step_dt():
                next(dgen, None)

            wx = [load_win(l, OFF["ax"] + 256 * i) for i in range(2)]
            pend_a = [None]
            for jc in range(4):
                ps, pk = bank()
                fm_mm(ps, pk, wx[jc // 2][0], wx[jc // 2][1], (jc % 2) * 128)
                tail = lru_tail[:, l, jc * 3:jc * 3 + 3]
                xa, xak = AR(jc)
                wcol = colsF[:, cb0 + 32 + jc * 4:cb0 + 36 + jc * 4]
                conv_chunk(ps, pk, tail, "lru_tail", wcol, xa, xak, jc)

                def fin_a(xa=xa, xak=xak, jc=jc):
                    P.op("act", lambda e: e.activation(out=xa, in_=xa, func=AF.Identity,
                                                       bias=colsF[:, cb0 + 48 + jc:cb0 + 49 + jc]),
                         r=xak + ["colsF"], w=xak)
                    xab, xabk = ARb(4 + jc)
                    P.op("act", lambda e: e.activation(out=xab[:, 0:T], in_=xa, func=AF.Copy), r=xak, w=xabk)
                if pend_a[0] is not None:
                    pend_a[0]()
                pend_a[0] = fin_a
                step_c()
            pend_a[0]()
            wg = [load_win(l, OFF["ag"] + 256 * i) for i in range(2)]
            for jc in range(4):
                xab, xabk = ARb(4 + jc)
                rr, rk = AR(8 + jc)
                ii, ik = AR(12 + jc)
                sg, sgk = AR(20 + jc)
                ps, pk = bank()
                P.op("pe", lambda e, ps=ps, jc=jc, xab=xab: e.matmul(ps[:, 0:T], lhsT=lrub[:, jc * 128:(jc + 1) * 128],
                                                                     rhs=xab[:, 0:T], start=True, stop=True),
                     r=xabk + ["lrub"], w=pk)
                P.op("act", lambda e, ps=ps, jc=jc, rr=rr: e.activation(out=rr, in_=ps[:, 0:T], func=AF.Tanh, scale=0.5,
                                                                        bias=dc[:, 24 + jc:25 + jc]),
                     r=pk + ["dcol_all"], w=rk)
                ps, pk = bank()
                P.op("pe", lambda e, ps=ps, jc=jc, xab=xab: e.matmul(ps[:, 0:T], lhsT=lrub[:, (4 + jc) * 128:(5 + jc) * 128],
                                                                     rhs=xab[:, 0:T], start=True, stop=True),
                     r=xabk + ["lrub"], w=pk)
                P.op("act", lambda e, ps=ps, jc=jc, ii=ii: e.activation(out=ii, in_=ps[:, 0:T], func=AF.Tanh, scale=0.5,
                                                                        bias=dc[:, 28 + jc:29 + jc]),
                     r=pk + ["dcol_all"], w=ik)
                ps, pk = bank()
                fm_mm(ps, pk, wg[jc // 2][0], wg[jc // 2][1], (jc % 2) * 128)
                P.op("act", lambda e, ps=ps, sg=sg: e.activation(out=sg, in_=ps[:, 0:T], func=AF.Silu), r=pk, w=sgk)
                step_c()
            for jc in range(4):
                xa, xak = AR(jc)
                rr, rk = AR(8 + jc)
                ii, ik = AR(12 + jc)
                aa, ak = AR(16 + jc)
                P.op("dve", lambda e, ii=ii, xa=xa: e.scalar_tensor_tensor(out=ii, in0=ii, scalar=1.0, in1=xa, op0=ALU.add,
                                                                          op1=ALU.mult), r=ik + xak, w=ik)
                P.op("act", lambda e, aa=aa, rr=rr, jc=jc: e.activation(out=aa, in_=rr, func=AF.Exp, scale=dc[:, 32 + jc:33 + jc],
                                                                        bias=dc[:, 32 + jc:33 + jc]), r=rk + ["dcol_all"], w=ak)
                P.op("act", lambda e, rr=rr, jc=jc: e.activation(out=rr, in_=rr, func=AF.Exp, scale=dc[:, 36 + jc:37 + jc],
                                                                 bias=dc[:, 36 + jc:37 + jc]), r=rk + ["dcol_all"], w=rk)
                P.op("act", lambda e, rr=rr: e.activation(out=rr, in_=rr, func=AF.Ln, scale=-1.0, bias=1.0), r=rk, w=rk)
                P.op("act", lambda e, rr=rr: e.activation(out=rr, in_=rr, func=AF.Exp, scale=0.5, bias=float(np.log(0.5))),
                     r=rk, w=rk)
            for jc in range(4):
                xa, xak = AR(jc)
                rr, rk = AR(8 + jc)
                ii, ik = AR(12 + jc)
                aa, ak = AR(16 + jc)
                sg, sgk = AR(20 + jc)
                P.op("dve", lambda e, ii=ii, rr=rr: e.tensor_tensor(out=ii, in0=ii, in1=rr, op=ALU.mult), r=ik + rk, w=ik)
                P.op("dve", lambda e, xa=xa, aa=aa, ii=ii, jc=jc: e.tensor_tensor_scan(out=xa, data0=aa, data1=ii,
                                                                                     initial=lru_h[:, l, jc:jc + 1],
                                                                                     op0=ALU.mult, op1=ALU.add),
                     r=ak + ik + ["lru_h"], w=xak)
                P.op("dve", lambda e, xa=xa, jc=jc: e.tensor_copy(out=lru_h[:, l, jc:jc + 1], in_=xa[:, T - 1:T]),
                     r=xak, w=["lru_h"])
                P.op("dve", lambda e, xa=xa, sg=sg, jc=jc: e.tensor_tensor(out=yT[:, jc, :], in0=xa, in1=sg, op=ALU.mult),
                     r=xak + sgk, w=[("yT", jc)])

            wgg = [load_win(l, OFF["g"] + 256 * i) for i in range(2)]
            vtm, vtk = ARb(8, 2)
            for jh in range(4):
                ps, pk = bank()
                fm_mm(ps, pk, wgg[jh // 2][0], wgg[jh // 2][1], (jh % 2) * 128)
                sgBb, sgBk = ARb(jh // 2)
                sgB = sgBb[:, (jh % 2) * 512:(jh % 2) * 512 + T]
                P.op("act", lambda e, ps=ps, sgB=sgB: e.activation(out=sgB, in_=ps[:, 0:T], func=AF.Silu), r=pk, w=sgBk)
                step_c()
            wi = [load_win(l, OFF["i"] + 256 * i) for i in range(2)]
            for tt in range(NT):
                ps, pk = bank()
                for hf_ in range(2):
                    tm_mm(ps, pk, wi[hf_][0], wi[hf_][1], tt, 0, 256, hf_ * 256)
                P.op("act", lambda e, ps=ps, tt=tt: e.activation(out=vtm[:, tt * 512:(tt + 1) * 512], in_=ps[:, 0:512],
                                                                 func=AF.Copy), r=pk, w=vtk)
                step_c()
            wq = [load_win(l, OFF["q"] + 256 * i) for i in range(2)]
            wf = [load_win(l, OFF["f"] + 256 * i) for i in range(2)]
            for jh in range(4):
                qq, qk = AR(10 + jh)
                fv, fk = AR(14 + jh)
                ps, pk = bank()
                fm_mm(ps, pk, wq[jh // 2][0], wq[jh // 2][1], (jh % 2) * 128)
                P.op("act", lambda e, ps=ps, qq=qq: e.activation(out=qq, in_=ps[:, 0:T], func=AF.Silu), r=pk, w=qk)
                ps, pk = bank()
                fm_mm(ps, pk, wf[jh // 2][0], wf[jh // 2][1], (jh % 2) * 128)
                P.op("act", lambda e, ps=ps, fv=fv: e.activation(out=fv, in_=ps[:, 0:T], func=AF.Tanh, scale=0.5), r=pk, w=fk)
                P.op("act", lambda e, fv=fv, jh=jh: e.activation(out=fv, in_=fv, func=AF.Identity, scale=dc[:, 44 + jh:45 + jh],
                                                                 bias=dc[:, 48 + jh:49 + jh]),
                     r=fk + ["dcol_all"], w=fk)
                step_c()
            for _ in cgen:
                pass
            HB = {}
            TSET = [
                [AR(18), AR(19), AR(20), AR(21), AR(22), AR(23)],
                [AR(2), AR(3), AR(38), AR(39), (nrm[:], ["nrm"]), (cbuf[0][:, 0:T], [("cbuf", 0)])],
            ]

            def gen_prep(jh, ts):
                qq, qk = AR(10 + jh)
                fv, fk = AR(14 + jh)
                (lf, lfk), (kk, kkk), (cum, cumk), (d1, d1k), (ek, ekk), (ec, eck) = TSET[ts]
                qkt, qktk = ARb(14 + jh)
                qhk, qhkk = ARb(24 + jh)
                qt = qkt[:, 0:T]
                kt = qkt[:, T:2 * T]
                qh = qhk[:, 0:T]
                khF = qhk[:, T:2 * T]
                ex8 = sm[:, 264 + 8 * jh:272 + 8 * jh]
                ex8k = ["sm_ex8_%d" % jh]
                kht = khtm4[jh]
                khtk = [("khtm", jh)]
                HB[jh] = dict(qt=qt, kt=kt, qh=qh, qktk=qktk, qhkk=qhkk, ex8=ex8, ex8k=ex8k, kht=kht, khtk=khtk)
                P.op("act", lambda e: e.activation(out=lf, in_=fv, func=AF.Ln), r=fk, w=lfk)
                P.op("act", lambda e: e.activation(out=kk, in_=fv, func=AF.Copy, scale=-1.0, bias=1.0), r=fk, w=kkk)
                yield
                P.op("dve", lambda e: e.tensor_tensor_scan(out=cum, data0=rmask[:], data1=lf, initial=0.0,
                                                           op0=ALU.mult, op1=ALU.add), r=lfk + ["rmask"], w=cumk)
                c3 = cum.rearrange("p (c j) -> p c j", j=64)
                P.op("dve", lambda e: e.tensor_tensor(out=d1.rearrange("p (c j) -> p c j", j=64), in0=c3,
                                                      in1=c3[:, :, 31:32].to_broadcast([128, 8, 64]),
                                                      op=ALU.subtract), r=cumk, w=d1k)
                P.op("dve", lambda e: e.tensor_tensor(out=lf.rearrange("p (c j) -> p c j", j=64), in0=c3,
                                                      in1=c3[:, :, 63:64].to_broadcast([128, 8, 64]),
                                                      op=ALU.subtract), r=cumk, w=lfk)
                yield
                P.op("act", lambda e: e.activation(out=ek, in_=d1, func=AF.Exp, scale=-1.0), r=d1k, w=ekk)
                P.op("act", lambda e: e.activation(out=d1, in_=d1, func=AF.Exp), r=d1k, w=d1k)
                P.op("act", lambda e: e.activation(out=lf, in_=lf, func=AF.Exp, scale=-1.0), r=lfk, w=lfk)
                P.op("act", lambda e: e.activation(out=ec, in_=cum, func=AF.Exp), r=cumk, w=eck)
                P.op("act", lambda e: e.activation(out=ex8, in_=c3[:, :, 63], func=AF.Exp), r=cumk, w=ex8k)
                yield
                P.op("dve", lambda e: e.tensor_tensor(out=kt, in0=kk, in1=ek, op=ALU.mult), r=kkk + ekk, w=qktk)
                P.op("dve", lambda e: e.tensor_tensor(out=qt, in0=qq, in1=d1, op=ALU.mult), r=qk + d1k, w=qktk)
                P.op("dve", lambda e: e.tensor_tensor(out=khF, in0=kk, in1=lf, op=ALU.mult), r=kkk + lfk, w=qhkk)
                P.op("dve", lambda e: e.tensor_tensor(out=qh, in0=qq, in1=ec, op=ALU.mult), r=qk + eck, w=qhkk)
                yield
                ps, pk = bank()
                psb = ps[:].bitcast(BF16)
                for tt in range(NT):
                    P.op("pe", lambda e, tt=tt: e.transpose(psb[:, tt * 128:(tt + 1) * 128],
                                                            khF[:, tt * 128:(tt + 1) * 128], identb[:]),
                         r=qhkk + ["identb"], w=pk)
                P.op("act", lambda e: e.activation(out=kht[:].rearrange("p a b -> p (a b)"), in_=psb[:, 0:512],
                                                   func=AF.Copy), r=pk, w=khtk)

            for pair in range(2):
                gl = [gen_prep(2 * pair, 0), gen_prep(2 * pair + 1, 1)]
                while gl:
                    for gg in list(gl):
                        try:
                            next(gg)
                        except StopIteration:
                            gl.remove(gg)
                    step_dt()
            for _ in dgen:
                pass
            psn, pnk = psum[7], [("ps", 7)]
            lb_i = [0]

            def lbank():
                i = lb_i[0]
                lb_i[0] = (i + 1) % 3
                return psum[i], [("ps", i)]
            for tt in range(NT):
                for jh in range(4):
                    h = HB[jh]
                    pss, psk = lbank()
                    P.op("pe", lambda e, tt=tt, pss=pss, h=h: e.matmul(pss[:, 0:128], lhsT=h["kt"][:, tt * 128:(tt + 1) * 128],
                                                                       rhs=h["qt"][:, tt * 128:(tt + 1) * 128], start=True, stop=True),
                         r=h["qktk"], w=psk)
                    pt = PT8[jh * 2 + tt % 2]
                    ptk = [("PT", jh * 2 + tt % 2)]
                    P.op("dve", lambda e, pt=pt, pss=pss: e.copy_predicated(out=pt[:], mask=maskbd[:], data=pss[:, 0:128]),
                         r=psk + ["maskbd"], w=ptk)
                for cc in range(2):
                    c = 2 * tt + cc
                    for jh in range(4):
                        h = HB[jh]
                        pso, pok = psum[3 + jh], [("ps", 3 + jh)]
                        if cc == 0:
                            pt = PT8[jh * 2 + tt % 2]
                            ptk = [("PT", jh * 2 + tt % 2)]
                            P.op("pe", lambda e, tt=tt, pt=pt, jh=jh, pso=pso: e.matmul(
                                pso[:, 0:128], lhsT=vtm[:, tt * 512 + jh * 128:tt * 512 + (jh + 1) * 128],
                                rhs=pt[:], start=True, stop=False), r=ptk + vtk, w=pok)
                        P.op("pe", lambda e, c=c, cc=cc, jh=jh, h=h, pso=pso: e.matmul(
                            pso[:, cc * 64:(cc + 1) * 64], lhsT=hgSb[:, jh * 128:(jh + 1) * 128],
                            rhs=h["qh"][:, c * 64:(c + 1) * 64], start=False, stop=(cc == 1), skip_group_check=True),
                            r=h["qhkk"] + [("hgSb", jh)], w=pok)
                        pkv, pkvk = lbank()
                        P.op("pe", lambda e, cc=cc, tt=tt, pkv=pkv, jh=jh, h=h: e.matmul(
                            pkv[:, 0:128], lhsT=h["kht"][cc * 64:(cc + 1) * 64, tt, :],
                            rhs=vtm[cc * 64:(cc + 1) * 64, tt * 512 + jh * 128:tt * 512 + (jh + 1) * 128],
                            start=True, stop=True), r=h["khtk"] + vtk, w=pkvk)
                        P.op("dve", lambda e, c=c, pkv=pkv, jh=jh, h=h: e.scalar_tensor_tensor(
                            out=hgSb[:, jh * 128:(jh + 1) * 128], in0=hgS[:, l, jh * 128:(jh + 1) * 128],
                            scalar=h["ex8"][:, c:c + 1], in1=pkv[:, 0:128], op0=ALU.mult, op1=ALU.add),
                            r=pkvk + [("hgS", jh)] + h["ex8k"], w=[("hgSb", jh)])
                        P.op("dve", lambda e, c=c, pkv=pkv, jh=jh, h=h: e.scalar_tensor_tensor(
                            out=hgS[:, l, jh * 128:(jh + 1) * 128], in0=hgS[:, l, jh * 128:(jh + 1) * 128],
                            scalar=h["ex8"][:, c:c + 1], in1=pkv[:, 0:128], op0=ALU.mult, op1=ALU.add),
                            r=pkvk + [("hgS", jh)] + h["ex8k"], w=[("hgS", jh)])
                for jh in range(4):
                    pso, pok = psum[3 + jh], [("ps", 3 + jh)]
                    osb, osbk = AR(4 + jh)
                    sqb, sqbk = ARb(22 + (jh % 2))
                    sqv = sqb[:, (jh // 2) * 128:(jh // 2) * 128 + 128]
                    P.op("act", lambda e, sqv=sqv, pso=pso: e.activation(out=sqv, in_=pso[:, 0:128], func=AF.Square), r=pok, w=sqbk)
                    P.op("act", lambda e, osb=osb, pso=pso, tt=tt, jh=jh: e.activation(
                        out=osb[:, tt * 128:(tt + 1) * 128], in_=pso[:, 0:128], func=AF.Copy,
                        scale=colsF[:, cb0 + 68 + jh:cb0 + 69 + jh]), r=pok + ["colsF"], w=osbk)
                    P.op("pe", lambda e, sqv=sqv, jh=jh, tt=tt: e.matmul(psn[:, tt * 128:(tt + 1) * 128], lhsT=onesb[:], rhs=sqv,
                                                                        start=(jh == 0), stop=(jh == 3)), r=sqbk + ["onesb"], w=pnk)
            P.op("act", lambda e: e.activation(out=nrm[:], in_=psn[:, 0:T], func=AF.Ln, scale=1.0 / 512, bias=EPS),
                 r=pnk, w=["nrm"])
            P.op("act", lambda e: e.activation(out=nrm[:], in_=nrm[:], func=AF.Exp, scale=-0.5), r=["nrm"], w=["nrm"])
            for jh in range(4):
                osb, osbk = AR(4 + jh)
                sgBb, sgBk = ARb(jh // 2)
                sgB = sgBb[:, (jh % 2) * 512:(jh % 2) * 512 + T]
                P.op("dve", lambda e, osb=osb: e.tensor_tensor(out=osb, in0=osb, in1=nrm[:], op=ALU.mult),
                     r=osbk + ["nrm"], w=osbk)
                P.op("dve", lambda e, osb=osb, sgB=sgB, jh=jh: e.tensor_tensor(out=yT[:, 4 + jh, :], in0=osb, in1=sgB, op=ALU.mult),
                     r=osbk + sgBk, w=[("yT", 4 + jh)])

            xstm, xstk = ARb(6, 4)
            btm, btk = ARb(10, 1)
            for tt in range(NT):
                ps, pk = bank()
                psb = ps[:].bitcast(BF16)
                for cc in range(8):
                    P.op("pe", lambda e, cc=cc, tt=tt, psb=psb: e.transpose(
                        psb[:, cc * 128:(cc + 1) * 128], xsT[:, cc * 512 + tt * 128:cc * 512 + (tt + 1) * 128], identb[:]),
                        r=xsTk + ["identb"], w=pk)
                P.op("act", lambda e, psb=psb, tt=tt: e.activation(out=xstm[:, tt * 1024:(tt + 1) * 1024], in_=psb[:, 0:1024],
                                                                   func=AF.Copy), r=pk, w=xstk)
            ps, pk = bank()
            psb = ps[:].bitcast(BF16)
            for tt in range(NT):
                for g in range(2):
                    P.op("pe", lambda e, g=g, tt=tt, psb=psb: e.transpose(
                        psb[:, tt * 256 + g * 128:tt * 256 + (g + 1) * 128], bcT[:, g * 512 + tt * 128:g * 512 + (tt + 1) * 128],
                        identb[:]), r=bcTk + ["identb"], w=pk)
            P.op("act", lambda e, psb=psb: e.activation(out=btm[:, 0:1024], in_=psb[:, 0:1024], func=AF.Copy), r=pk, w=btk)
            def smv(base, tt, g=None):
                o = base + 16 * tt
                if g is None:
                    return sm[:, o:o + 16]
                return sm[:, o + 8 * g:o + 8 * g + 8]
            CUMC, EXPC, NEGC, EXL = 296, 360, 424, 488
            MSLOT = [21, 22, 23, 24, 32, 33, 34, 35]
            XDSLOT = [25, 26, 38, 39]
            XSSLOT = [2, 3, 4, 5]

            def p1_common(tt):
                cumc = smv(CUMC, tt)
                expc = smv(EXPC, tt)
                ps, pk = bank()
                P.op("pe", lambda e, ps=ps: e.matmul(ps[:, 0:16], lhsT=tri, rhs=dAt[:, tt * 16:(tt + 1) * 16],
                                                     start=True, stop=True), r=["cf", "sm_dA"], w=pk)
                P.op("act", lambda e, ps=ps: e.activation(out=cumc, in_=ps[:, 0:16], func=AF.Copy), r=pk, w=[("sm_cumc", tt)])
                P.op("act", lambda e: e.activation(out=expc, in_=cumc, func=AF.Exp), r=[("sm_cumc", tt)], w=[("sm_expc", tt)])
                negc = smv(NEGC, tt)
                P.op("dve", lambda e: e.tensor_scalar(out=negc, in0=cumc, scalar1=-1.0, scalar2=None, op0=ALU.mult),
                     r=[("sm_cumc", tt)], w=[("sm_negc", tt)])

            def gen_p1(tt, g):
                p2 = tt % 2
                cumc = smv(CUMC, tt)
                expc = smv(EXPC, tt)
                R, Rk = AR((17 if tt % 2 == 0 else 11) + 2 * g, 2)
                R3 = R.rearrange("p (h t) -> p h t", t=128)
                decs = R3[:, :, 127]
                negc = smv(NEGC, tt)
                exl = smv(EXL, tt, g)
                pb = [bank(), bank()]
                for hh in range(8):
                    pbh = pb[hh // 4][0][:, (hh % 4) * 128:(hh % 4 + 1) * 128]
                    col = tt * 16 + g * 8 + hh
                    P.op("pe", lambda e, pbh=pbh, col=col: e.matmul(
                        pbh, lhsT=dAt[:, col:col + 1].to_broadcast([128, 128]), rhs=tri, start=True, stop=False),
                        r=["cf", "sm_dA"], w=pb[hh // 4][1])
                    P.op("pe", lambda e, pbh=pbh, hh=hh: e.matmul(
                        pbh, lhsT=ident, rhs=negc[:, g * 8 + hh:g * 8 + hh + 1].to_broadcast([128, 128]), start=False, stop=True),
                        r=["cf", ("sm_negc", tt)], w=pb[hh // 4][1])
                for hf_ in range(2):
                    P.op("act", lambda e, hf_=hf_: e.activation(out=R[:, hf_ * 512:(hf_ + 1) * 512], in_=pb[hf_][0][:, 0:512],
                                                                func=AF.Exp), r=pb[hf_][1], w=Rk)
                yield
                P.op("dve", lambda e: e.tensor_tensor(out=exl, in0=decs, in1=expc[:, g * 8:(g + 1) * 8], op=ALU.mult),
                     r=Rk + [("sm_expc", tt)], w=[("sm_exl", tt, g)])
                psg, psgk = bank()
                P.op("pe", lambda e: e.matmul(
                    psg[:, 0:128], lhsT=bcT[:, g * 512 + tt * 128:g * 512 + (tt + 1) * 128],
                    rhs=bcT[:, (2 + g) * 512 + tt * 128:(2 + g) * 512 + (tt + 1) * 128], start=True, stop=True),
                    r=bcTk, w=psgk)
                cbm = cbuf[g][:, 0:128]
                cbmk = [("cbuf", g)]
                P.op("dve", lambda e: e.tensor_tensor(out=cbm, in0=psg[:, 0:128], in1=tri, op=ALU.mult),
                     r=psgk + ["cf"], w=cbmk)
                yield
                Mb, Mbk = ARb(MSLOT[tt * 2 + g])
                P.op("dve", lambda e: e.scalar_tensor_tensor(
                    out=Mb[:, 0:1024].rearrange("p (h t) -> p h t", t=128), in0=R3, scalar=1.0,
                    in1=cbm.unsqueeze(1).to_broadcast([128, 8, 128]), op0=ALU.min, op1=ALU.mult),
                    r=Rk + cbmk, w=Mbk)
                yield
                xdtb, xdtk = ARb(XDSLOT[tt])
                xdt = xdtb[:, g * 512:(g + 1) * 512]
                xs_g = xstm[:, tt * 1024 + g * 512:tt * 1024 + (g + 1) * 512].rearrange("p (h q) -> p h q", q=64)
                P.op("dve", lambda e: e.tensor_tensor(
                    out=xdt.rearrange("p (h q) -> p h q", q=64), in0=xs_g,
                    in1=dtt[:, tt * 16 + g * 8:tt * 16 + (g + 1) * 8].unsqueeze(2).to_broadcast([128, 8, 64]), op=ALU.mult),
                    r=xstk + ["sm_dt"], w=xdtk)
                xscb, xsck = ARb(XSSLOT[tt])
                xsc = xscb[:, g * 512:(g + 1) * 512]
                P.op("dve", lambda e: e.tensor_tensor(
                    out=xsc.rearrange("p (h q) -> p h q", q=64), in0=xdt.rearrange("p (h q) -> p h q", q=64),
                    in1=decs.unsqueeze(2).to_broadcast([128, 8, 64]), op=ALU.mult),
                    r=xdtk + Rk, w=xsck)

            def gen_p2(tt, g):
                p2 = tt % 2
                expc = smv(EXPC, tt)
                yc, yck = ARb(27)
                exl = smv(EXL, tt, g)
                xscb, xsck = ARb(XSSLOT[tt])
                xsc = xscb[:, g * 512:(g + 1) * 512]
                Mb, Mbk = ARb(MSLOT[tt * 2 + g])
                xdtb, xdtk = ARb(XDSLOT[tt])
                xdt = xdtb[:, g * 512:(g + 1) * 512]
                xs_g = xstm[:, tt * 1024 + g * 512:tt * 1024 + (g + 1) * 512].rearrange("p (h q) -> p h q", q=64)
                t1, t1k = AR(0 + g)
                t13 = t1.rearrange("p (h q) -> p h q", q=64)
                xD, xDk = AR(15 + g)
                sv = ssS[:, l, g * 512:(g + 1) * 512]
                ssq = sm[:, 256 + g:257 + g]
                psi, psik = bank()
                P.op("pe", lambda e: e.matmul(
                    psi[:, 0:512], lhsT=bcT[:, (2 + g) * 512 + tt * 128:(2 + g) * 512 + (tt + 1) * 128],
                    rhs=ssSb[:, g * 512:(g + 1) * 512], start=True, stop=True), r=bcTk + [("ssSb", g)], w=psik)
                P.op("dve", lambda e: e.tensor_tensor(
                    out=t13, in0=psi[:, 0:512].rearrange("p (h q) -> p h q", q=64),
                    in1=expc[:, g * 8:(g + 1) * 8].unsqueeze(2).to_broadcast([128, 8, 64]), op=ALU.mult),
                    r=psik + [("sm_expc", tt)], w=t1k)
                yield
                P.op("dve", lambda e: e.tensor_tensor(
                    out=sv.rearrange("p (h q) -> p h q", q=64), in0=sv.rearrange("p (h q) -> p h q", q=64),
                    in1=exl.unsqueeze(2).to_broadcast([128, 8, 64]), op=ALU.mult),
                    r=[("ssS", g), ("sm_exl", tt, g)], w=[("ssS", g)])
                pst, pstk = bank()
                P.op("pe", lambda e: e.matmul(
                    pst[:, 0:512], lhsT=btm[:, tt * 256 + g * 128:tt * 256 + (g + 1) * 128], rhs=xsc, start=True, stop=True),
                    r=btk + xsck, w=pstk)
                P.op("dve", lambda e: e.tensor_tensor(out=sv, in0=sv, in1=pst[:, 0:512], op=ALU.add),
                     r=[("ssS", g)] + pstk, w=[("ssS", g)])
                yield
                P.op("act", lambda e: e.activation(out=ssSb[:, g * 512:(g + 1) * 512], in_=sv, func=AF.Copy),
                     r=[("ssS", g)], w=[("ssSb", g)])
                yield
                psy, psyk = bank()
                for hh in range(8):
                    P.op("pe", lambda e, hh=hh: e.matmul(
                        psy[:, hh * 64:(hh + 1) * 64], lhsT=Mb[:, hh * 128:(hh + 1) * 128], rhs=xdt[:, hh * 64:(hh + 1) * 64],
                        start=True, stop=True), r=Mbk + xdtk, w=psyk)
                P.op("dve", lambda e: e.tensor_tensor(out=t1, in0=t1, in1=psy[:, 0:512], op=ALU.add),
                     r=t1k + psyk, w=t1k)
                yield
                P.op("dve", lambda e: e.tensor_tensor(
                    out=xD.rearrange("p (h q) -> p h q", q=64), in0=xs_g,
                    in1=rowb[:, 1056 + g * 8:1056 + (g + 1) * 8].unsqueeze(2).to_broadcast([128, 8, 64]), op=ALU.mult),
                    r=xstk + ["rowb"], w=xDk)
                P.op("dve", lambda e: e.tensor_tensor(out=t1, in0=t1, in1=xD, op=ALU.add), r=t1k + xDk, w=t1k)
                P.op("dve", lambda e: e.tensor_tensor(
                    out=t1, in0=t1, in1=sztm[:, tt * 1024 + g * 512:tt * 1024 + (g + 1) * 512], op=ALU.mult),
                    r=t1k + szk, w=t1k)
                yield
                P.op("act", lambda e: e.activation(out=xD, in_=t1, func=AF.Square, accum_out=ssq),
                     r=t1k, w=xDk + [("sm_ssq", g)])
                P.op("act", lambda e: e.activation(out=ssq, in_=ssq, func=AF.Ln, scale=1.0 / 512, bias=EPS),
                     r=[("sm_ssq", g)], w=[("sm_ssq", g)])
                P.op("act", lambda e: e.activation(out=ssq, in_=ssq, func=AF.Exp, scale=-0.5),
                     r=[("sm_ssq", g)], w=[("sm_ssq", g)])
                P.op("act", lambda e: e.activation(out=t1, in_=t1, func=AF.Copy, scale=ssq),
                     r=t1k + [("sm_ssq", g)], w=t1k)
                yield
                P.op("dve", lambda e: e.tensor_tensor(out=yc[:, g * 512:(g + 1) * 512], in0=t1,
                                                      in1=rowb[:, g * 512:(g + 1) * 512], op=ALU.mult),
                     r=t1k + ["rowb"], w=yck)

            def p2_final(tt):
                yc, yck = ARb(27)
                ps, pk = bank()
                psb = ps[:].bitcast(BF16)
                for kc in range(8):
                    P.op("pe", lambda e, kc=kc: e.transpose(psb[:, kc * 128:(kc + 1) * 128],
                                                            yc[:, kc * 128:(kc + 1) * 128], identb[:]),
                         r=yck + ["identb"], w=pk)
                P.op("act", lambda e: e.activation(out=yT[:, 8:16, tt * 128:(tt + 1) * 128],
                                                   in_=psb[:, 0:1024].rearrange("p (a b) -> p a b", b=128),
                                                   func=AF.Copy), r=pk, w=[("yT", c) for c in range(8, 16)])

            def run_rr(gens):
                gens = list(gens)
                while gens:
                    for gg in list(gens):
                        try:
                            next(gg)
                        except StopIteration:
                            gens.remove(gg)

            p1_common(0)
            run_rr([gen_p1(0, 0), gen_p1(0, 1)])
            ahead = {}
            for tt in range(1, NT):
                p1_common(tt)
                ahead[tt] = [gen_p1(tt, 0), gen_p1(tt, 1)]

            def step_all(gl):
                for gg in list(gl):
                    try:
                        next(gg)
                    except StopIteration:
                        gl.remove(gg)
            for tt in range(NT):
                cur = [gen_p2(tt, 0), gen_p2(tt, 1)]
                while cur:
                    step_all(cur)
                    for t2 in range(tt + 1, NT):
                        if ahead[t2]:
                            step_all(ahead[t2])
                            break
                if tt + 1 < NT:
                    while ahead[tt + 1]:
                        step_all(ahead[tt + 1])
                p2_final(tt)

            if dbg and j == nseg - 1 and l == nlayers - 1:
                P.dma("sp", dbg_y[:, :], yT[:].rearrange("p a b -> p (a b)"), r=ykeys, w=["dbg_y"])

            for dm in range(8):
                wo, wok = load_wout(l, dm)
                ps, pk = bank()
                for kc in range(16):
                    P.op("pe", lambda e, kc=kc, ps=ps, wo=wo: e.matmul(ps[:, 0:T], lhsT=wo[:, kc, :], rhs=yT[:, kc, :],
                                                                       start=(kc == 0), stop=(kc == 15)),
                         r=wok + [("yT", kc)], w=pk)
                P.op("dve", lambda e, dm=dm, ps=ps: e.scalar_tensor_tensor(out=xT[:, dm, :], in0=ps[:, 0:T],
                                                                           scalar=dcol[:, l, 16 + dm:17 + dm], in1=xT[:, dm, :],
                                                                           op0=ALU.mult, op1=ALU.add),
                     r=pk + [("xT", dm), "dcol_all"], w=[("xT", dm)])

        def final_out(j):
            norm_phase(colsF[:, CB:CB + 8], None, False)
            for tt in range(NT):
                ob, obk = AR(8 + 2 * (tt % 2), 2)
                for hf_ in range(2):
                    ps, pk = bank()
                    for q4 in range(4):
                        c = hf_ * 4 + q4
                        src, sk = AR(c)
                        P.op("pe", lambda e, q4=q4, tt=tt, ps=ps, src=src: e.transpose(ps[:, q4 * 128:(q4 + 1) * 128],
                                                                                      src[:, tt * 128:(tt + 1) * 128], ident),
                             r=sk + ["cf"], w=pk)
                    P.op("act" if hf_ else "dve",
                         (lambda e, ps=ps, ob=ob, hf_=hf_: e.activation(out=ob[:, hf_ * 512:(hf_ + 1) * 512], in_=ps[:, 0:512], func=AF.Copy))
                         if hf_ else
                         (lambda e, ps=ps, ob=ob, hf_=hf_: e.tensor_copy(out=ob[:, hf_ * 512:(hf_ + 1) * 512], in_=ps[:, 0:512])),
                         r=pk, w=obk)
                P.dma("sp", out_d[j * T + tt * 128:j * T + (tt + 1) * 128, :], ob, r=obk, w=[("out", j, tt)])

        P._commit(("dve", P.cnt["dve"]), [], ["dcol_all"])

        for j in range(nseg):
            if j == 0:
                load_x_dma(0)
            load_x_tr(j)
            for l in range(nlayers):
                unit(j, l)
            if dbg and j == nseg - 1:
                P.dma("sp", dbg_x[:, :], xT[:].rearrange("p a b -> p (a b)"), r=xkeys, w=["dbg_x"])
            if j + 1 < nseg:
                load_x_dma(j + 1)
            final_out(j)

        okeys = [("out", j, tt) for j in range(nseg) for tt in range(NT)]
        if dbg:
            okeys += ["dbg_x", "dbg_y"]
        P.wait_all("sp", okeys)
        for i in range(P.NDS):
            if P.dval[i] > 0:
                P._wait("sp", {(("d", i), P.dval[i])})
        print("instr counts", P.cnt)
    return nc


def _host_layout(inputs):
    f32 = np.float32
    g = {k: np.asarray(v, f32) for k, v in inputs.items()}

    def fm(v, n):
        return np.ascontiguousarray(v.reshape(n, 128).T)

    cols_shared = np.zeros((128, NCOLS), f32)
    for l in range(DEPTH):
        b = l * LC
        cols_shared[:, b + 0:b + 8] = fm(g["norm_w"][l], 8)
        cols_shared[:, b + 8:b + 32] = fm(g["b_ada"][l], 24)
        for jc in range(4):
            for k in range(4):
                cols_shared[:, b + 32 + jc * 4 + k] = g["lru_conv_w"][l, k, jc * 128:(jc + 1) * 128]
        cols_shared[:, b + 48:b + 52] = fm(g["lru_conv_b"][l], 4)
        cols_shared[:, b + 52:b + 56] = fm(g["lru_ba"][l].reshape(512), 4)
        cols_shared[:, b + 56:b + 60] = fm(g["lru_bx"][l].reshape(512), 4)
        cols_shared[:, b + 60:b + 64] = fm(g["lru_lambda"][l], 4)
        cols_shared[:, b + 64:b + 68] = fm(g["hg_lb_logits"][l], 4)
        cols_shared[:, b + 68:b + 72] = fm(g["hg_norm_w"][l], 4)
        for cc in range(12):
            for k in range(4):
                cols_shared[:, b + 72 + cc * 4 + k] = g["ssd_conv_w"][l, k, cc * 128:(cc + 1) * 128]
        cols_shared[:, b + 120:b + 132] = fm(g["ssd_conv_b"][l], 12)
    CBb = LC * DEPTH
    cols_shared[:, CBb:CBb + 8] = fm(g["final_norm_w"], 8)
    rowp = np.zeros((DEPTH, 1072), f32)
    rowp[:, 0:1024] = g["ssd_norm_w"]
    rowp[:, 1024:1040] = g["ssd_dt_bias"]
    rowp[:, 1040:1056] = g["ssd_a_log"]
    rowp[:, 1056:1072] = g["ssd_d"]
    lrubd = np.zeros((DEPTH, 128, 8, 128), f32)
    for l in range(DEPTH):
        for m, nm in enumerate(("lru_wa", "lru_wx")):
            for jc in range(4):
                for hh in range(2):
                    lrubd[l, hh * 64:(hh + 1) * 64, m * 4 + jc, hh * 64:(hh + 1) * 64] = g[nm][l, jc * 2 + hh]
    lrubd = lrubd.reshape(DEPTH, 128, 1024)
    constf = np.zeros((128, 384), f32)
    constf[:, 0:128] = np.eye(128, dtype=f32)
    constf[:, 128:256] = np.triu(np.ones((128, 128), f32))
    constf[:, 256:384] = 1.0
    idx = np.arange(128)
    constm = ((idx[:, None] <= idx[None, :]) & ((idx[:, None] // 64) == (idx[None, :] // 64))).astype(np.uint32)
    in_maps = []
    for b in range(g["x"].shape[0]):
        cols = cols_shared.copy()
        cols[:, CBb + 8:CBb + 16] = fm(g["c"][b], 8)
        in_maps.append({"x": np.ascontiguousarray(g["x"][b]), "w_ada": g["w_ada"], "w_in": g["w_in"], "w_out": g["w_out"],
                        "colsF": cols, "rowp": rowp, "lrubd": lrubd, "constf": constf, "constm": constm})
    return in_maps


_NC_CACHE = {}


def kernel(**inputs):
    in_maps = _host_layout(inputs)
    if "nc" not in _NC_CACHE:
        _NC_CACHE["nc"] = _build()
    nc = _NC_CACHE["nc"]
    res = run_bass_kernel_spmd(nc, in_maps, core_ids=list(range(8)))
    out = np.stack([np.asarray(r["out"], np.float32) for r in res.results], axis=0)
    return out
```
